# Optimizing a Trainium2 kernel written in Bass

```python
import math
import jax
import jax.numpy as jnp
from jax import lax
import numpy as np

D_MODEL = 2048
BATCH = 4
SEQ = 4096
DEPTH = 1

CHUNK = 64
Q_BLOCK = 128
SB_HEADS = 8
SB_HEAD_DIM = 128
SB_WIDTH = SB_HEADS * SB_HEAD_DIM
DF_HEADS = 4
DF_HEAD_DIM = 128
DF_V_DIM = 2 * DF_HEAD_DIM
DF_QK_WIDTH = DF_HEADS * 2 * DF_HEAD_DIM
DF_V_WIDTH = DF_HEADS * DF_V_DIM
D_FF = int(math.ceil(8 * D_MODEL / 3 / 256) * 256)
IN_WIDTH = 3 * SB_WIDTH + 2 * DF_QK_WIDTH + DF_V_WIDTH + 2 * D_MODEL
EPS = 1e-6
SUBLN_EPS = 1e-5

kernel_name = "hybrid_stickbreak_diffattn_gated_block"


def rms_norm(x, g, eps=EPS):
    xf = x.astype(jnp.float32)
    y = xf * lax.rsqrt(jnp.mean(xf * xf, axis=-1, keepdims=True) + eps)
    return (y * g.astype(jnp.float32)).astype(x.dtype)


def alibi_slopes(n_heads):
    return jnp.exp2(-8.0 * (jnp.arange(n_heads, dtype=jnp.float32) + 1.0) / n_heads)


def to_blocks(t):
    b, h, s, d = t.shape
    return t.reshape(b, h, s // Q_BLOCK, Q_BLOCK, d).transpose(2, 0, 1, 3, 4)


def from_blocks(t):
    nb, b, h, q, d = t.shape
    return t.transpose(1, 2, 0, 3, 4).reshape(b, h, nb * q, d)


def stick_breaking_attention(q, k, v):
    s_len, d = q.shape[2], q.shape[3]
    scale = 1.0 / math.sqrt(d)
    kpos = jnp.arange(s_len)

    def one_block(args):
        q_blk, i = args
        z = jnp.einsum('bhqd,bhkd->bhqk', q_blk, k).astype(jnp.float32) * scale
        qpos = i * Q_BLOCK + jnp.arange(Q_BLOCK)
        strict = kpos[None, :] < qpos[:, None]
        log_beta = jax.nn.log_sigmoid(z)
        log_one_minus = jnp.where(strict, jax.nn.log_sigmoid(-z), 0.0)
        tail = lax.cumsum(log_one_minus, axis=3, reverse=True) - log_one_minus
        w = jnp.where(strict, jnp.exp(log_beta + tail), 0.0)
        return jnp.einsum('bhqk,bhkd->bhqd', w.astype(v.dtype), v)

    nb = s_len // Q_BLOCK
    out = lax.map(one_block, (to_blocks(q), jnp.arange(nb)))
    return from_blocks(out)


def differential_attention(q1, q2, k1, k2, v, lam):
    n_heads, s_len, d = q1.shape[1], q1.shape[2], q1.shape[3]
    scale = 1.0 / math.sqrt(d)
    slopes = alibi_slopes(n_heads)
    kpos = jnp.arange(s_len)

    def one_block(args):
        q1_blk, q2_blk, i = args
        qpos = i * Q_BLOCK + jnp.arange(Q_BLOCK)
        dist = jnp.abs(qpos[:, None] - kpos[None, :]).astype(jnp.float32)
        allowed = (kpos[None, :] // CHUNK) <= (qpos[:, None] // CHUNK)
        bias = jnp.where(allowed[None], -slopes[:, None, None] * dist[None], -jnp.inf)
        s1 = jnp.einsum('bhqd,bhkd->bhqk', q1_blk, k1).astype(jnp.float32) * scale + bias
        s2 = jnp.einsum('bhqd,bhkd->bhqk', q2_blk, k2).astype(jnp.float32) * scale + bias
        p = jax.nn.softmax(s1, axis=-1) - lam * jax.nn.softmax(s2, axis=-1)
        return jnp.einsum('bhqk,bhkd->bhqd', p.astype(v.dtype), v)

    nb = s_len // Q_BLOCK
    out = lax.map(one_block, (to_blocks(q1), to_blocks(q2), jnp.arange(nb)))
    return from_blocks(out)


def setup_inputs(seed: int = 0) -> dict:
    key = jax.random.key(seed)
    ks = jax.random.split(key, 18)
    f32 = jnp.float32

    def w(k, fan_in, fan_out):
        return jax.random.normal(k, (DEPTH, fan_in, fan_out), f32) * fan_in ** -0.5

    def gain(k, n):
        return 1.0 + 0.01 * jax.random.normal(k, (DEPTH, n), f32)

    return {
        "x": jax.random.normal(ks[0], (BATCH, SEQ, D_MODEL), f32),
        "norm1_g": gain(ks[1], D_MODEL),
        "w_in": w(ks[2], D_MODEL, IN_WIDTH),
        "q_norm_g": gain(ks[3], DF_HEAD_DIM),
        "k_norm_g": gain(ks[4], DF_HEAD_DIM),
        "lambda_q1": 0.1 * jax.random.normal(ks[5], (DEPTH, DF_HEAD_DIM), f32),
        "lambda_k1": 0.1 * jax.random.normal(ks[6], (DEPTH, DF_HEAD_DIM), f32),
        "lambda_q2": 0.1 * jax.random.normal(ks[7], (DEPTH, DF_HEAD_DIM), f32),
        "lambda_k2": 0.1 * jax.random.normal(ks[8], (DEPTH, DF_HEAD_DIM), f32),
        "subln_g": gain(ks[9], DF_V_DIM),
        "w_branch_a": w(ks[10], SB_WIDTH, D_MODEL),
        "w_branch_b": w(ks[11], DF_V_WIDTH, D_MODEL),
        "w_out": w(ks[12], D_MODEL, D_MODEL),
        "norm2_g": gain(ks[13], D_MODEL),
        "w_ffn_gate": w(ks[14], D_MODEL, D_FF),
        "w_ffn_up": w(ks[15], D_MODEL, D_FF),
        "w_ffn_down": w(ks[16], D_FF, D_MODEL),
    }


def reference(x, norm1_g, w_in, q_norm_g, k_norm_g, lambda_q1, lambda_k1, lambda_q2,
              lambda_k2, subln_g, w_branch_a, w_branch_b, w_out, norm2_g,
              w_ffn_gate, w_ffn_up, w_ffn_down):
    b, s, _ = x.shape
    split_points = np.cumsum([SB_WIDTH, SB_WIDTH, SB_WIDTH, DF_QK_WIDTH, DF_QK_WIDTH,
                              DF_V_WIDTH, D_MODEL]).tolist()
    for layer in range(DEPTH):
        lambda_init = 0.8 - 0.6 * math.exp(-0.3 * layer)

        xn = rms_norm(x, norm1_g[layer])
        proj = jnp.einsum('bsd,de->bse', xn, w_in[layer])
        sb_q, sb_k, sb_v, df_q, df_k, df_v, gate_a, gate_b = jnp.split(proj, split_points, axis=-1)

        heads_a = lambda t: t.reshape(b, s, SB_HEADS, SB_HEAD_DIM).transpose(0, 2, 1, 3)
        out_a = stick_breaking_attention(heads_a(sb_q), heads_a(sb_k), heads_a(sb_v))
        out_a = out_a.transpose(0, 2, 1, 3).reshape(b, s, SB_WIDTH)

        dq = df_q.reshape(b, s, DF_HEADS, 2, DF_HEAD_DIM).transpose(3, 0, 2, 1, 4)
        dk = df_k.reshape(b, s, DF_HEADS, 2, DF_HEAD_DIM).transpose(3, 0, 2, 1, 4)
        dq = rms_norm(dq, q_norm_g[layer])
        dk = rms_norm(dk, k_norm_g[layer])
        dv = df_v.reshape(b, s, DF_HEADS, DF_V_DIM).transpose(0, 2, 1, 3)
        lam = (jnp.exp(jnp.sum(lambda_q1[layer] * lambda_k1[layer]).astype(jnp.float32))
               - jnp.exp(jnp.sum(lambda_q2[layer] * lambda_k2[layer]).astype(jnp.float32))
               + lambda_init)
        out_b = differential_attention(dq[0], dq[1], dk[0], dk[1], dv, lam)
        out_b = rms_norm(out_b, subln_g[layer], SUBLN_EPS) * (1.0 - lambda_init)
        out_b = out_b.transpose(0, 2, 1, 3).reshape(b, s, DF_V_WIDTH)

        merged = (jax.nn.sigmoid(gate_a) * jnp.einsum('bse,ed->bsd', out_a, w_branch_a[layer])
                  + jax.nn.sigmoid(gate_b) * jnp.einsum('bse,ed->bsd', out_b, w_branch_b[layer]))
        x = x + jnp.einsum('bsd,de->bse', merged, w_out[layer])

        hn = rms_norm(x, norm2_g[layer])
        hidden = (jax.nn.silu(jnp.einsum('bsd,df->bsf', hn, w_ffn_gate[layer]))
                  * jnp.einsum('bsd,df->bsf', hn, w_ffn_up[layer]))
        x = x + jnp.einsum('bsf,fd->bsd', hidden, w_ffn_down[layer])
    return x
```

```python
import contextlib
import math
import numpy as np
import concourse.bass as bass
import concourse.mybir as mybir
from concourse.bass_utils import run_bass_kernel_spmd

F32 = mybir.dt.float32
BF16 = mybir.dt.bfloat16
AF = mybir.ActivationFunctionType
ALU = mybir.AluOpType
AX = mybir.AxisListType

D = 2048
SEQ = 4096
NTOK = 2048
DFF = 5632
NFC = DFF // 128
EPS = 1e-6
SUBLN_EPS = 1e-5
SCALE = 1.0 / math.sqrt(128.0)
LAMBDA_INIT = 0.8 - 0.6 * math.exp(-0.3 * 0)
NEG = -30000.0


class Op:
    __slots__ = ("eng", "fn", "dma", "deps", "idx", "sig", "dsem", "dval", "n")

    def __init__(self, eng, fn, dma):
        self.eng = eng; self.fn = fn; self.dma = dma
        self.deps = (); self.idx = None; self.sig = False
        self.dsem = None; self.dval = None; self.n = None


class Sched:
    KDMA = 8
    ENGS = ("sp", "pe", "act", "dve", "pool")
    QUEUES = ("sp",)

    def __init__(self, nc, stack):
        self.nc = nc
        self.ops = []
        self.kw = {}
        self.kr = {}
        self.esem = {e: stack.enter_context(nc.semaphore("s_" + e)) for e in self.ENGS}
        self.dsems = {q: [stack.enter_context(nc.semaphore(f"d_{q}{i}")) for i in range(self.KDMA)]
                      for q in self.QUEUES}
        self.last_c = {}
        self.last_d = {q: [] for q in self.QUEUES}

    def op(self, eng, fn, reads=(), writes=(), dma=False, deps=()):
        o = Op(eng, fn, dma)
        d = set(deps)
        kw = self.kw; kr = self.kr
        for k in reads:
            w = kw.get(k)
            if w is not None:
                d.add(w)
        for k in writes:
            w = kw.get(k)
            if w is not None:
                d.add(w)
            r = kr.get(k)
            if r:
                d.update(r)
        d.discard(o)
        o.deps = d
        for k in reads:
            kr.setdefault(k, []).append(o)
        for k in writes:
            kw[k] = o
            kr[k] = []
        self.ops.append(o)
        if dma:
            l = self.last_d[eng]
            l.append(o)
            if len(l) > self.KDMA:
                l.pop(0)
        elif fn is not None:
            self.last_c[eng] = o
        return o

    def barrier(self):
        lasts = list(self.last_c.values())
        for q in self.QUEUES:
            lasts.extend(self.last_d[q])
        for e in self.ENGS:
            self.op(e, None, deps=lasts)
        self.kw.clear(); self.kr.clear()

    def emit(self, block):
        for o in self.ops:
            for d in o.deps:
                if d.dma:
                    continue
                if d.eng == "pe" and o.eng == "pe" and not o.dma:
                    continue
                d.sig = True
        cnt = {e: 0 for e in self.ENGS}
        dcnt = {q: 0 for q in self.QUEUES}
        for o in self.ops:
            if o.dma:
                n = dcnt[o.eng]; dcnt[o.eng] += 1
                o.n = n
                o.dsem = self.dsems[o.eng][n % self.KDMA]
                o.dval = 16 * (n // self.KDMA + 1)
            elif o.sig:
                cnt[o.eng] += 1
                o.idx = cnt[o.eng]
        for e in cnt:
            assert cnt[e] < 60000, (e, cnt[e])
        for q in dcnt:
            assert 16 * (dcnt[q] // self.KDMA + 1) < 60000
        self.counts = (cnt, dcnt)
        per = {e: [o for o in self.ops if o.eng == e] for e in self.ENGS}

        def run(ename):
            def body(eng):
                waited = {}

                def wait(sem, val):
                    key = id(sem)
                    if waited.get(key, 0) >= val:
                        return
                    waited[key] = val
                    eng.wait_ge(sem, val)
                for o in per[ename]:
                    for d in o.deps:
                        if d.dma:
                            wait(d.dsem, d.dval)
                        else:
                            if d.eng == "pe" and ename == "pe" and not o.dma:
                                continue
                            wait(self.esem[d.eng], d.idx)
                    if o.dma:
                        if o.n >= self.KDMA:
                            wait(o.dsem, o.dval - 16)
                        o.fn(eng).then_inc(o.dsem, 16)
                    elif o.fn is not None:
                        ins = o.fn(eng)
                        if o.sig:
                            ins.then_inc(self.esem[ename], 1)
            return body
        block.sync(run("sp"))
        block.tensor(run("pe"))
        block.scalar(run("act"))
        block.vector(run("dve"))
        block.gpsimd(run("pool"))


class Arena:
    def __init__(self, ap, base_bytes, limit_bytes):
        self.ap = ap; self.off = base_bytes; self.limit = limit_bytes

    def _take(self, nbytes):
        nbytes = (nbytes + 63) // 64 * 64
        o = self.off
        self.off += nbytes
        assert self.off <= self.limit, ("arena overflow", self.off, self.limit)
        return o

    def f32(self, ncols):
        o = self._take(ncols * 4)
        return self.ap[:, o // 4: o // 4 + ncols]

    def bf16(self, ncols):
        assert ncols % 2 == 0
        o = self._take(ncols * 2)
        return self.ap[:, o // 4: o // 4 + ncols // 2].bitcast(BF16)


ARENA_KB = 206


def build_program(debug=False):
    nc = bass.Bass("TRN2", target_bir_lowering=False)
    dt_in = lambda n, s, d=F32: nc.dram_tensor(n, s, d, kind="ExternalInput").ap()
    xoT = dt_in("xoT", [D, NTOK])
    xfT = dt_in("xfT", [D, SEQ])
    w_in = dt_in("w_in", [D, 10240])
    w_a = dt_in("w_a", [1024, D])
    w_b = dt_in("w_b", [1024, D])
    w_o = dt_in("w_o", [D, D])
    w_g = dt_in("w_g", [D, DFF])
    w_u = dt_in("w_u", [D, DFF])
    w_d = dt_in("w_d", [DFF, D])
    g1_d = dt_in("g1", [128, 16])
    g2_d = dt_in("g2", [128, 16])
    gq_d = dt_in("gq", [128, 1])
    gk_d = dt_in("gk", [128, 1])
    lamv_d = dt_in("lamv", [128, 512])
    gsub_d = dt_in("gsub", [128, 256])
    ebias_d = dt_in("ebias", [4, 128, 4096])
    sbmask_d = dt_in("sbmask", [128, 256])
    outT = nc.dram_tensor("outT", [D, NTOK], F32, kind="ExternalOutput").ap()
    skind = "ExternalOutput" if debug else "Internal"
    qT = nc.dram_tensor("qT", [16, 128, NTOK], BF16, kind=skind).ap()
    kT = nc.dram_tensor("kT", [16, 128, SEQ], BF16, kind=skind).ap()
    vA = nc.dram_tensor("vA", [8, 128, SEQ], BF16, kind=skind).ap()
    vD = nc.dram_tensor("vD", [4, 128, 2 * SEQ], BF16, kind=skind).ap()
    gT = nc.dram_tensor("gT", [32, 128, NTOK], BF16, kind=skind).ap()
    hT = nc.dram_tensor("hT", [16, 128, NTOK], F32, kind=skind).ap()
    oaD = nc.dram_tensor("oaD", [128, 8 * NTOK], BF16, kind=skind).ap() if debug else None
    obD = nc.dram_tensor("obD", [128, 8 * NTOK], BF16, kind=skind).ap() if debug else None

    with contextlib.ExitStack() as st:
        S = Sched(nc, st)
        arena_t = st.enter_context(nc.sbuf_tensor("arena", [128, ARENA_KB * 256], F32))
        arena = arena_t[:]
        LIM = ARENA_KB * 1024
        ps = [st.enter_context(nc.psum_tensor(f"ps{i}", [128, 512], F32)) for i in range(8)]
        psf = [p[:] for p in ps]
        psb = [p[:].bitcast(BF16) for p in ps]
        block = st.enter_context(nc.Block())

        C = Arena(arena, 0, LIM)
        ident = C.bf16(128)
        ones = C.f32(528)
        g1 = C.f32(16); g2 = C.f32(16); gq = C.f32(2); gk = C.f32(2)
        gsub = C.f32(256); sbmask = C.f32(256)
        nlam = C.f32(2)
        rstd2 = C.f32(NTOK)
        maskneg = C.bf16(256)
        smallbase = C.off

        uid = [0]

        def key(n):
            uid[0] += 1
            return (n, uid[0])

        def cast2(dst, src, kd, ks, K):
            h2 = K // 2
            S.op("dve", lambda e: e.tensor_copy(out=dst[:, 0:h2, :], in_=src[:, 0:h2, :]), reads=[ks], writes=[kd + (0,)])
            S.op("act", lambda e: e.copy(out=dst[:, h2:K, :], in_=src[:, h2:K, :]), reads=[ks], writes=[kd + (1,)])

        def load(dst, src, k, q="sp"):
            return S.op(q, lambda e: e.dma_start(out=dst, in_=src), writes=[k], dma=True)

        load(g1, g1_d, "g1"); load(g2, g2_d, "g2")
        load(gq[:, 0:1], gq_d, "gq"); load(gk[:, 0:1], gk_d, "gk")
        load(gsub, gsub_d, "gsub"); load(sbmask, sbmask_d, "sbmask")
        S.op("pool", lambda e: e.memset(ident, 0.0), writes=["ident"])
        S.op("pool", lambda e: e.affine_select(out=ident, in_=ident, pattern=[[-1, 128]],
                                               compare_op=ALU.not_equal, fill=1.0, base=0,
                                               channel_multiplier=1), reads=["ident"], writes=["ident"])
        S.op("dve", lambda e: e.memset(ones, 1.0), writes=["ones"])
        S.op("dve", lambda e: e.tensor_scalar_mul(out=maskneg, in0=sbmask, scalar1=NEG), reads=["sbmask"], writes=["maskneg"])
        S.op("dve", lambda e: e.tensor_scalar_mul(out=gsub, in0=gsub, scalar1=1.0 - LAMBDA_INIT),
             reads=["gsub"], writes=["gsub"])
        T = Arena(arena, smallbase, LIM)
        lamv = T.f32(512); lt = T.f32(256); ls = T.f32(4)
        load(lamv, lamv_d, "lamv")
        for i in range(2):
            S.op("dve", lambda e, i=i: e.tensor_tensor(out=lt[:, i * 128:(i + 1) * 128],
                                                       in0=lamv[:, i * 256:i * 256 + 128],
                                                       in1=lamv[:, i * 256 + 128:i * 256 + 256], op=ALU.mult),
                 reads=["lamv"], writes=[("lt", i)])
            S.op("dve", lambda e, i=i: e.reduce_sum(out=ls[:, i:i + 1], in_=lt[:, i * 128:(i + 1) * 128], axis=AX.X),
                 reads=[("lt", i)], writes=[("ls", i)])
            S.op("act", lambda e, i=i: e.activation(out=ls[:, 2 + i:3 + i], in_=ls[:, i:i + 1], func=AF.Exp),
                 reads=[("ls", i)], writes=[("le", i)])
        S.op("dve", lambda e: e.tensor_tensor(out=nlam[:, 0:1], in0=ls[:, 3:4], in1=ls[:, 2:3], op=ALU.subtract),
             reads=[("le", 0), ("le", 1)], writes=["nlam"])
        S.op("dve", lambda e: e.tensor_scalar_add(out=nlam[:, 0:1], in0=nlam[:, 0:1], scalar1=-LAMBDA_INIT),
             reads=["nlam"], writes=["nlam"])
        S.barrier()

        A = Arena(arena, smallbase, LIM)
        XTOP = LIM - 64 * 1024
        AT = Arena(arena, XTOP, LIM)
        xn = [AT.bf16(16 * 1024).rearrange("p (k t) -> p k t", k=16) for _ in range(2)]
        A.limit = XTOP
        xs = A.f32(16 * 512).rearrange("p (k t) -> p k t", k=16)
        sq = [A.f32(512) for _ in range(2)]
        rs = A.f32(512)
        nacc = A.f32(512)
        wst = [A.f32(16 * 256).rearrange("p (k c) -> p k c", k=16) for _ in range(2)]
        wbf = [A.bf16(16 * 256).rearrange("p (k c) -> p k c", k=16) for _ in range(2)]
        sq2 = [A.f32(512) for _ in range(2)]
        rs2 = [A.f32(512) for _ in range(2)]
        obuf = [A.bf16(512) for _ in range(4)]
        w_in_v = w_in.rearrange("(k p) c -> p k c", p=128)
        xoT_v = xoT.rearrange("(k p) t -> p k t", p=128)
        xfT_v = xfT.rearrange("(k p) t -> p k t", p=128)

        rot = {"psA": 0, "ob": 0, "st2": 0, "w": 0}

        def norm_dma(src_v, c0, tg):
            S.op("sp", lambda e: e.dma_start(out=xs, in_=src_v[:, :, c0 + tg * 512: c0 + (tg + 1) * 512]),
                 writes=["xs"], dma=True)

        def norm_sq():
            S.op("pool", lambda e: e.tensor_tensor(out=nacc, in0=xs[:, 0, :], in1=xs[:, 0, :], op=ALU.mult),
                 reads=["xs"], writes=["nacc"])
            for k in range(1, 16):
                S.op("pool", lambda e, k=k: e.tensor_tensor(out=sq[0], in0=xs[:, k, :], in1=xs[:, k, :], op=ALU.mult),
                     reads=["xs"], writes=[("sq", 0)])
                S.op("pool", lambda e, k=k: e.tensor_tensor(out=nacc, in0=nacc, in1=sq[0], op=ALU.add),
                     reads=[("sq", 0), "nacc"], writes=["nacc"])

        def norm_part2(pi, tg):
            pp = pi % 2
            S.op("pe", lambda e: e.matmul(psf[4], lhsT=ones[:, 0:128], rhs=nacc, start=True, stop=True),
                 reads=["nacc"], writes=["ps4"])
            S.op("act", lambda e: e.activation(out=rs, in_=psf[4], func=AF.Ln, scale=1.0 / D, bias=EPS),
                 reads=["ps4"], writes=["rs"])
            S.op("act", lambda e: e.activation(out=rs, in_=rs, func=AF.Exp, scale=-0.5),
                 reads=["rs"], writes=["rs"])
            for k in range(16):
                S.op("dve", lambda e, k=k: e.scalar_tensor_tensor(
                    out=xn[pp][:, k, tg * 512:(tg + 1) * 512], in0=xs[:, k, :], scalar=g1[:, k:k + 1],
                    in1=rs, op0=ALU.mult, op1=ALU.mult),
                    reads=["xs", "rs", "g1"], writes=[("xn", pp, tg, k)])

        pending = []

        def flush():
            for f in pending:
                f()
            del pending[:]

        def qknorm_evac(psi, gcol, ob, kps, dst):
            s2 = rot["st2"] % 2; rot["st2"] += 1
            stp = 5 + s2
            S.op("act", lambda e: e.activation(out=sq2[s2], in_=psf[psi], func=AF.Square),
                 reads=[kps], writes=[("sq2", s2)])

            def tail():
                S.op("pe", lambda e: e.matmul(psf[stp], lhsT=ones[:, 0:128], rhs=sq2[s2], start=True, stop=True),
                     reads=[("sq2", s2), "ones"], writes=[("ps", stp)])
                S.op("act", lambda e: e.activation(out=rs2[s2], in_=psf[stp], func=AF.Ln, scale=1.0 / 128, bias=EPS),
                     reads=[("ps", stp)], writes=[("rs2", s2)])
                S.op("act", lambda e: e.activation(out=rs2[s2], in_=rs2[s2], func=AF.Exp, scale=-0.5),
                     reads=[("rs2", s2)], writes=[("rs2", s2)])
                S.op("dve", lambda e: e.scalar_tensor_tensor(out=obuf[ob], in0=psf[psi], scalar=gcol, in1=rs2[s2],
                                                             op0=ALU.mult, op1=ALU.mult),
                     reads=[kps, ("rs2", s2)], writes=[("ob", ob)])
                S.op("sp", lambda e: e.dma_start(out=dst, in_=obuf[ob]), reads=[("ob", ob)], dma=True)
            pending.append(tail)

        def compute_tile(pi, tile, tok0, s):
            pp = pi % 2
            if True:
                if True:
                    c0, kind, ch0, _dst = tile
                    if kind == "v":
                        for tb in range(8):
                            psi = rot["psA"] % 4; rot["psA"] += 1
                            for k in range(16):
                                S.op("pe", lambda e, k=k, tb=tb, psi=psi, s=s: e.matmul(
                                    psf[psi][:, 0:256], lhsT=xn[pp][:, k, tb * 128:(tb + 1) * 128], rhs=wbf[s][:, k, :],
                                    start=(k == 0), stop=(k == 15)),
                                    reads=[("wbf", s, 0), ("wbf", s, 1), ("xn", pp, tb // 4, k)], writes=[("ps", psi)])
                            flush()
                            ob = rot["ob"] % 4; rot["ob"] += 1
                            S.op("dve", lambda e, psi=psi, ob=ob: e.tensor_copy(out=obuf[ob][:, 0:256], in_=psf[psi][:, 0:256]),
                                 reads=[("ps", psi)], writes=[("ob", ob)])
                            blk = (tok0 + tb * 128) // 128
                            if ch0 < 1024:
                                h0 = ch0 // 128
                                dstv = vA[h0:h0 + 2].rearrange("h p c -> p h c")[:, :, blk * 128:(blk + 1) * 128]
                                srcv = obuf[ob][:, 0:256].rearrange("p (h c) -> p h c", h=2)
                            else:
                                hd = (ch0 - 1024) // 256
                                dstv = vD[hd][:, blk * 256:(blk + 1) * 256]
                                srcv = obuf[ob][:, 0:256]
                            S.op("sp", lambda e, dstv=dstv, srcv=srcv: e.dma_start(out=dstv, in_=srcv),
                                 reads=[("ob", ob)], dma=True)
                        return
                    for ch in range(2):
                        for tg in range(2):
                            psi = rot["psA"] % 4; rot["psA"] += 1
                            for k in range(16):
                                S.op("pe", lambda e, k=k, ch=ch, tg=tg, psi=psi, s=s: e.matmul(
                                    psf[psi], lhsT=wbf[s][:, k, ch * 128:(ch + 1) * 128],
                                    rhs=xn[pp][:, k, tg * 512:(tg + 1) * 512], start=(k == 0), stop=(k == 15)),
                                    reads=[("wbf", s, 0), ("wbf", s, 1), ("xn", pp, tg, k)], writes=[("ps", psi)])
                            flush()
                            ob = rot["ob"] % 4; rot["ob"] += 1
                            kps = ("ps", psi)
                            dst = _dst[ch0 + ch][:, tok0 + tg * 512: tok0 + (tg + 1) * 512]
                            if kind == "sb":
                                S.op("dve", lambda e, psi=psi, ob=ob: e.tensor_copy(out=obuf[ob], in_=psf[psi]),
                                     reads=[kps], writes=[("ob", ob)])
                            elif kind == "gate":
                                S.op("act", lambda e, psi=psi, ob=ob: e.activation(out=obuf[ob], in_=psf[psi], func=AF.Sigmoid),
                                     reads=[kps], writes=[("ob", ob)])
                            elif kind == "dfq":
                                qknorm_evac(psi, gq[:, 0:1], ob, kps, dst)
                                continue
                            elif kind == "dfk":
                                qknorm_evac(psi, gk[:, 0:1], ob, kps, dst)
                                continue
                            S.op("sp", lambda e, ob=ob, dst=dst: e.dma_start(out=dst, in_=obuf[ob]),
                                 reads=[("ob", ob)], dma=True)


        own_tiles = []
        for t in range(4):
            own_tiles.append((t * 256, "sb", 2 * t, qT))
        for t in range(4):
            own_tiles.append((3072 + t * 256, "dfq", 8 + 2 * t, qT))
        kv_tiles = []
        for t in range(4):
            kv_tiles.append((1024 + t * 256, "sb", 2 * t, kT))
        for t in range(4):
            kv_tiles.append((4096 + t * 256, "dfk", 8 + 2 * t, kT))
        for t in range(4):
            kv_tiles.append((2048 + t * 256, "v", t * 256, None))
        for t in range(4):
            kv_tiles.append((5120 + t * 256, "v", 1024 + t * 256, None))

        passes = [(xfT_v, i * 1024, kv_tiles) for i in range(4)] + \
                 [(xoT_v, 0, own_tiles), (xoT_v, 1024, own_tiles)]
        def norm_first(src_v, c0, tg, pp=0):
            for k in range(16):
                S.op("sp", lambda e, k=k: e.dma_start(out=xs[:, k, :], in_=src_v[:, k, c0 + tg * 512: c0 + (tg + 1) * 512]),
                     writes=[("xs1", k)] + (["xs"] if k == 0 else []), reads=([] if k == 0 else ["xs"]), dma=True)
            for k in range(16):
                S.op("act", lambda e, k=k: e.activation(out=sq[k % 2], in_=xs[:, k, :], func=AF.Square),
                     reads=[("xs1", k)], writes=[("sq", k % 2)])
                S.op("pe", lambda e, k=k: e.matmul(psf[4], lhsT=ones[:, 0:128], rhs=sq[k % 2], start=(k == 0), stop=(k == 15)),
                     reads=[("sq", k % 2)], writes=["ps4"])
            S.op("act", lambda e: e.activation(out=rs, in_=psf[4], func=AF.Ln, scale=1.0 / D, bias=EPS),
                 reads=["ps4"], writes=["rs"])
            S.op("act", lambda e: e.activation(out=rs, in_=rs, func=AF.Exp, scale=-0.5),
                 reads=["rs"], writes=["rs"])
            for k in range(16):
                S.op("dve", lambda e, k=k: e.scalar_tensor_tensor(
                    out=xn[pp][:, k, tg * 512:(tg + 1) * 512], in0=xs[:, k, :], scalar=g1[:, k:k + 1],
                    in1=rs, op0=ALU.mult, op1=ALU.mult),
                    reads=[("xs1", k), "rs", "g1"], writes=[("xn", pp, tg, k), "xs"])

        for tg in range(2):
            norm_first(passes[0][0], passes[0][1], tg)
        G = []
        for pi, (src_v, c0, tiles) in enumerate(passes):
            hooks = {}
            if pi + 1 < len(passes) and len(tiles) < 14:
                nsrc, nc0, _ = passes[pi + 1]
                hooks[0] = [lambda nsrc=nsrc, nc0=nc0, pi=pi: norm_first(nsrc, nc0, 0, (pi + 1) % 2)]
                hooks[4] = [lambda nsrc=nsrc, nc0=nc0, pi=pi: norm_first(nsrc, nc0, 1, (pi + 1) % 2)]
            elif pi + 1 < len(passes):
                nsrc, nc0, _ = passes[pi + 1]
                hooks[0] = [lambda nsrc=nsrc, nc0=nc0: norm_dma(nsrc, nc0, 0)]
                hooks[2] = [norm_sq]
                hooks[5] = [lambda pi=pi: norm_part2(pi + 1, 0)]
                hooks[8] = [lambda nsrc=nsrc, nc0=nc0: norm_dma(nsrc, nc0, 1)]
                hooks[10] = [norm_sq]
                hooks[13] = [lambda pi=pi: norm_part2(pi + 1, 1)]
            for li, tile in enumerate(tiles):
                G.append((pi, li, tile, c0, hooks))
        NG = len(G)
        for i in range(NG + 2):
            if i < NG:
                c0w = G[i][2][0]
                s = i % 2
                S.op("sp", lambda e, s=s, c0w=c0w: e.dma_start(out=wst[s], in_=w_in_v[:, :, c0w:c0w + 256]),
                     writes=[("wst", s)], dma=True)
            if 0 <= i - 1 < NG:
                s = (i - 1) % 2
                cast2(wbf[s], wst[s], ("wbf", s), ("wst", s), 16)
            if 0 <= i - 2 < NG:
                pi, li, tile, tok0, hooks = G[i - 2]
                for hk in hooks.get(li, ()):
                    hk()
                compute_tile(pi, tile, tok0, (i - 2) % 2)
        flush()
        S.barrier()

        P = Arena(arena, smallbase, LIM)
        oaT = P.bf16(8 * NTOK).rearrange("p (h t) -> p h t", h=8)
        b1base = P.off
        obT = P.bf16(8 * NTOK).rearrange("p (h t) -> p h t", h=8)
        attbase = P.off
        mT = P.bf16(16 * NTOK).rearrange("p (k t) -> p k t", k=16)
        cbase = P.off
        B = Arena(arena, b1base, XTOP)
        kTh = [B.bf16(SEQ) for _ in range(2)]
        vh = [B.bf16(32 * 128).rearrange("p (b d) -> p b d", b=32) for _ in range(2)]
        qTh = [B.bf16(NTOK) for _ in range(2)]
        RB = 5
        OMr = [B.f32(528) for _ in range(RB)]
        PIr = [B.f32(528) for _ in range(RB)]
        Wr = [B.bf16(512) for _ in range(RB)]
        WTr = [B.bf16(512).rearrange("p (b q) -> p b q", b=4) for _ in range(RB)]
        gwst = B.f32(16 * 128).rearrange("p (k c) -> p k c", k=16)
        gwbf = [B.bf16(16 * 128).rearrange("p (k c) -> p k c", k=16) for _ in range(2)]
        gob = [B.bf16(512) for _ in range(4)]

        def b1_loads(h):
            hb = h % 2
            S.op("sp", lambda e: e.dma_start(out=kTh[hb], in_=kT[h]), writes=[("kTh", hb)], dma=True)
            S.op("sp", lambda e: e.dma_start(out=qTh[hb], in_=qT[h]), writes=[("qTh", hb)], dma=True)
            S.op("sp", lambda e: e.dma_start(out=vh[hb], in_=vA[h].rearrange("p (b d) -> p b d", b=32)),
                 writes=[("vh", hb)], dma=True)

        chunks = []
        rowi = 0
        for h in range(8):
            li = 0
            for j in range(16):
                L = 256 * (j + 1)
                offs = list(range(0, L, 512))
                for ci, off in enumerate(offs):
                    chunks.append(dict(h=h, j=j, off=off, w=min(512, L - off), i0=SEQ - L, first=(ci == 0),
                                       last=(ci == len(offs) - 1), nblk=L // 128, pso=4 + rowi % 2, li=li))
                    li += 1
                rowi += 1
        NCH = len(chunks)

        def b1_st0(n):
            c = chunks[n]; sl = n % RB; pss = n % 2
            h, j, off, w, i0 = c["h"], c["j"], c["off"], c["w"], c["i0"]
            hb = h % 2
            if c["li"] == 8 and h + 1 < 8:
                b1_loads(h + 1)
            fst = c["first"]
            S.op("pe", lambda e: e.matmul(psf[pss][:, 0:w], lhsT=qTh[hb][:, j * 128:(j + 1) * 128],
                                          rhs=kTh[hb][:, i0 + off:i0 + off + w], start=True, stop=(not fst)),
                 reads=[("qTh", hb), ("kTh", hb)], writes=[("ps", pss)])
            if fst:
                S.op("pe", lambda e: e.matmul(psf[pss][:, 0:256], lhsT=ident, rhs=maskneg, start=False, stop=True),
                     writes=[("ps", pss)])
            S.op("act", lambda e: e.activation(out=OMr[sl][:, 1:1 + w], in_=psf[pss][:, 0:w], func=AF.Sigmoid, scale=-SCALE),
                 reads=[("ps", pss)], writes=[("OM", sl)])
            if c["first"]:
                S.op("dve", lambda e: e.tensor_tensor_scan(out=PIr[sl][:, 0:w + 1], data0=OMr[sl][:, 0:w + 1],
                                                           data1=ones[:, 0:1].to_broadcast([128, w + 1]), initial=1.0, op0=ALU.mult, op1=ALU.mult),
                     reads=[("OM", sl)], writes=[("PI", sl)])
            else:
                psl = (n - 1) % RB
                S.op("dve", lambda e: e.tensor_tensor_scan(out=PIr[sl][:, 0:w + 1], data0=OMr[sl][:, 0:w + 1],
                                                           data1=ones[:, 0:1].to_broadcast([128, w + 1]), initial=PIr[psl][:, 512:513],
                                                           op0=ALU.mult, op1=ALU.mult),
                     reads=[("OM", sl), ("PI", psl)], writes=[("PI", sl)])
            S.op("pool", lambda e: e.tensor_tensor(out=Wr[sl][:, 0:w], in0=PIr[sl][:, 0:w], in1=PIr[sl][:, 1:1 + w],
                                                   op=ALU.subtract), reads=[("PI", sl)], writes=[("W", sl)])

        def b1_st1(n):
            c = chunks[n]; sl = n % RB; pst = 2 + n % 2
            w = c["w"]; nb = w // 128
            for bi in range(nb):
                S.op("pe", lambda e, bi=bi: e.transpose(psb[pst][:, bi * 128:(bi + 1) * 128],
                                                        Wr[sl][:, bi * 128:(bi + 1) * 128], ident),
                     reads=[("W", sl)], writes=[("ps", pst)])
            S.op("act", lambda e: e.copy(out=WTr[sl][:, 0:nb, :],
                                         in_=psb[pst][:, 0:w].rearrange("p (b q) -> p b q", b=nb)),
                 reads=[("ps", pst)], writes=[("WT", sl)])

        def b1_st2(n):
            c = chunks[n]; sl = n % RB
            h, j, off, w, i0, pso, nblk = c["h"], c["j"], c["off"], c["w"], c["i0"], c["pso"], c["nblk"]
            hb = h % 2; nb = w // 128; b0 = off // 128
            for bi in range(nb):
                gb = b0 + bi
                S.op("pe", lambda e, bi=bi, gb=gb: e.matmul(psf[pso][:, 0:128], lhsT=vh[hb][:, i0 // 128 + gb, :],
                                                            rhs=WTr[sl][:, bi, :], start=(gb == 0), stop=(gb == nblk - 1)),
                     reads=[("vh", hb), ("WT", sl)], writes=[("ps", pso)])
            if c["last"]:
                S.op("act", lambda e: e.copy(out=oaT[:, h, j * 128:(j + 1) * 128], in_=psf[pso][:, 0:128]),
                     reads=[("ps", pso)], writes=[("oaT", h, j)])

        for i in range(RB):
            S.op("pool", lambda e, i=i: e.memset(OMr[i][:, 0:1], 1.0), writes=[("OM", i)])
        gev = []
        gcnt = {"ob": 0}

        def g_load(g):
            c0 = 6144 + (g % 32) * 128
            S.op("sp", lambda e: e.dma_start(out=gwst, in_=w_in_v[:, :, c0:c0 + 128]), writes=["gwst"], dma=True)

        def g_cast(g):
            s_ = g % 2
            S.op("act", lambda e: e.copy(out=gwbf[s_], in_=gwst), reads=["gwst"], writes=[("gwbf", s_)])

        def g_grp(g, tg):
            half = g // 32; t = g % 32; s_ = g % 2
            gb = 6 + tg
            for k in range(16):
                S.op("pe", lambda e, k=k: e.matmul(psf[gb], lhsT=gwbf[s_][:, k, :], rhs=xn[half][:, k, tg * 512:(tg + 1) * 512],
                                                   start=(k == 0), stop=(k == 15)),
                     reads=[("gwbf", s_)], writes=[("ps", gb)])
            ob = gcnt["ob"] % 4; gcnt["ob"] += 1
            S.op("act", lambda e: e.activation(out=gob[ob], in_=psf[gb], func=AF.Sigmoid), reads=[("ps", gb)], writes=[("gob", ob)])
            c_lo = half * 1024 + tg * 512
            S.op("sp", lambda e: e.dma_start(out=gT[t][:, c_lo:c_lo + 512], in_=gob[ob]), reads=[("gob", ob)], dma=True)

        NGT = 64
        gev.append([lambda: g_load(0), lambda: g_cast(0), lambda: g_load(1)])
        for g in range(NGT):
            gev.append([lambda g=g: g_grp(g, 0), lambda g=g: g_grp(g, 1)])
            u = []
            if g + 1 < NGT:
                u.append(lambda g=g: g_cast(g + 1))
            if g + 2 < NGT:
                u.append(lambda g=g: g_load(g + 2))
            if u:
                gev.append(u)
        gdone = [0]

        def g_pump(n):
            span = NCH - 40
            tgt = len(gev) if n >= span else min(len(gev), 1 + ((n + 1) * (len(gev) - 1)) // span)
            while gdone[0] < tgt:
                for f in gev[gdone[0]]:
                    f()
                gdone[0] += 1

        b1_loads(0)
        for i in range(NCH + 4):
            if i < NCH:
                g_pump(i)
                b1_st0(i)
            if 0 <= i - 3 < NCH:
                b1_st1(i - 3)
            if 0 <= i - 4 < NCH:
                b1_st2(i - 4)
        S.barrier()

        B = Arena(arena, attbase, LIM)
        kaT = [[B.bf16(SEQ) for _ in range(2)] for _ in range(2)]
        qaT = [[B.bf16(NTOK) for _ in range(2)] for _ in range(2)]
        Va = [B.bf16(32 * 258).rearrange("p (b d) -> p b d", b=32) for _ in range(2)]
        Et = [B.f32(4096) for _ in range(2)]
        RX = 4
        ex = [B.f32(512) for _ in range(2)]
        PT = [B.bf16(512) for _ in range(RX)]
        d1 = [B.f32(256) for _ in range(2)]
        dd = [B.f32(256) for _ in range(4)]
        ob16 = [B.bf16(256) for _ in range(4)]
        sm = [B.f32(8) for _ in range(4)]
        for i in range(2):
            S.op("pool", lambda e, i=i: e.memset(Va[i], 1.0), writes=[("Va", i)])

        def b2_loads(h):
            hb = h % 2
            for a in range(2):
                S.op("sp", lambda e, a=a: e.dma_start(out=kaT[hb][a], in_=kT[8 + 2 * h + a]), writes=[("kaT", hb, a)], dma=True)
                S.op("sp", lambda e, a=a: e.dma_start(out=qaT[hb][a], in_=qT[8 + 2 * h + a]), writes=[("qaT", hb, a)], dma=True)
            vDv = vD[h].rearrange("p (b d) -> p b d", b=32)
            for q4 in range(4):
                S.op("sp", lambda e, q4=q4: e.dma_start(out=Va[hb][:, q4 * 8:(q4 + 1) * 8, 0:256], in_=vDv[:, q4 * 8:(q4 + 1) * 8, :]),
                     writes=[("Va", hb, q4)], reads=[("Va", hb)], dma=True)
            S.op("sp", lambda e: e.dma_start(out=Et[hb], in_=ebias_d[h]), writes=[("Et", hb)], dma=True)

        def b2_etexp(h):
            hb = h % 2
            for c8 in range(8):
                S.op("act", lambda e, c8=c8: e.activation(out=Et[hb][:, c8 * 512:(c8 + 1) * 512],
                                                          in_=Et[hb][:, c8 * 512:(c8 + 1) * 512], func=AF.Exp),
                     reads=[("Et", hb)], writes=[("Et", hb)])

        bats = []
        rowi = 0
        for h in range(4):
            li = 0
            for j in range(16):
                nblk = 2 * j + 2
                for a in range(2):
                    m0s = list(range(0, nblk, 4))
                    for m0 in m0s:
                        bats.append(dict(h=h, j=j, a=a, m0=m0, nb=min(4, nblk - m0), nblk=nblk, eb=rowi % 2,
                                         last=(a == 1 and m0 == m0s[-1]), li=li, r4=rowi % 4))
                        li += 1
                rowi += 1
        NB = len(bats)
        deferred = {}

        def b2_st0(n):
            c = bats[n]; sl = n % RX; pss = n % 2
            h, j, a, m0, nb, nblk = c["h"], c["j"], c["a"], c["m0"], c["nb"], c["nblk"]
            hb = h % 2; i0b = 32 - nblk
            if c["li"] == 12 and h + 1 < 4:
                b2_loads(h + 1)
            if c["li"] == 40 and h + 1 < 4:
                b2_etexp(h + 1)
            for mi in range(nb):
                cb = (i0b + m0 + mi) * 128
                S.op("pe", lambda e, mi=mi, cb=cb: e.matmul(psf[pss][:, mi * 128:(mi + 1) * 128], lhsT=kaT[hb][a][:, cb:cb + 128],
                                                            rhs=qaT[hb][a][:, j * 128:(j + 1) * 128], start=True, stop=True),
                     reads=[("kaT", hb, a), ("qaT", hb, a)], writes=[("ps", pss)])
            S.op("act", lambda e: e.activation(out=ex[pss][:, 0:nb * 128], in_=psf[pss][:, 0:nb * 128], func=AF.Exp, scale=SCALE),
                 reads=[("ps", pss)], writes=[("ex", pss)])
            S.op("dve", lambda e: e.tensor_tensor(out=PT[sl][:, 0:nb * 128], in0=ex[pss][:, 0:nb * 128],
                                                  in1=Et[hb][:, m0 * 128:(m0 + nb) * 128], op=ALU.mult),
                 reads=[("ex", pss), ("Et", hb)], writes=[("PT", sl)])

        def b2_epi(c, it):
            h, j, eb, r4 = c["h"], c["j"], c["eb"], c["r4"]
            p1 = 4 + 2 * eb; p2 = p1 + 1
            smv = sm[r4]
            S.op("dve", lambda e: e.reciprocal(out=smv[:, 0:1], in_=psf[p1][:, 256:257]),
                 reads=[("ps", p1)], writes=[("sm", r4, 0)])
            S.op("dve", lambda e: e.reciprocal(out=smv[:, 1:2], in_=psf[p2][:, 256:257]),
                 reads=[("ps", p2)], writes=[("sm", r4, 1)])
            S.op("dve", lambda e: e.tensor_tensor(out=smv[:, 2:3], in0=smv[:, 1:2], in1=nlam[:, 0:1], op=ALU.mult),
                 reads=[("sm", r4, 1)], writes=[("sm", r4, 2)])
            S.op("dve", lambda e: e.tensor_scalar_mul(out=d1[eb], in0=psf[p1][:, 0:256], scalar1=smv[:, 0:1]),
                 reads=[("ps", p1), ("sm", r4, 0)], writes=[("d1", eb)])
            S.op("dve", lambda e: e.scalar_tensor_tensor(out=dd[r4], in0=psf[p2][:, 0:256], scalar=smv[:, 2:3], in1=d1[eb],
                                                         op0=ALU.mult, op1=ALU.add),
                 reads=[("ps", p2), ("sm", r4, 2), ("d1", eb)], writes=[("dd", r4)])

            def e2():
                S.op("act", lambda e: e.activation(out=d1[eb], in_=dd[r4], func=AF.Square, accum_out=smv[:, 3:4]),
                     reads=[("dd", r4)], writes=[("d1", eb), ("sm", r4, 3)])
                S.op("act", lambda e: e.activation(out=smv[:, 4:5], in_=smv[:, 3:4], func=AF.Ln, scale=1.0 / 256, bias=SUBLN_EPS),
                     reads=[("sm", r4, 3)], writes=[("sm", r4, 4)])
                S.op("act", lambda e: e.activation(out=smv[:, 5:6], in_=smv[:, 4:5], func=AF.Exp, scale=-0.5),
                     reads=[("sm", r4, 4)], writes=[("sm", r4, 5)])

            def e3():
                S.op("dve", lambda e: e.scalar_tensor_tensor(out=ob16[r4], in0=dd[r4], scalar=smv[:, 5:6], in1=gsub,
                                                             op0=ALU.mult, op1=ALU.mult),
                     reads=[("dd", r4), ("sm", r4, 5)], writes=[("ob16", r4)])

            def e4():
                pst = 2 + eb
                for hf in range(2):
                    S.op("pe", lambda e, hf=hf: e.transpose(psb[pst][:, hf * 128:(hf + 1) * 128],
                                                            ob16[r4][:, hf * 128:(hf + 1) * 128], ident),
                         reads=[("ob16", r4)], writes=[("ps", pst)])
                S.op("act", lambda e: e.copy(out=obT[:, 2 * h:2 * h + 2, j * 128:(j + 1) * 128],
                                             in_=psb[pst][:, 0:256].rearrange("p (b q) -> p b q", b=2)),
                     reads=[("ps", pst)], writes=[("obT", h, j)])
            deferred.setdefault(it + 2, []).append(e2)
            deferred.setdefault(it + 4, []).append(e3)
            deferred.setdefault(it + 6, []).append(e4)

        def b2_st1(n, it):
            c = bats[n]; sl = n % RX
            h, a, m0, nb, nblk, eb = c["h"], c["a"], c["m0"], c["nb"], c["nblk"], c["eb"]
            hb = h % 2; i0b = 32 - nblk
            pso = 4 + 2 * eb + a
            for mi in range(nb):
                gb = m0 + mi
                S.op("pe", lambda e, mi=mi, gb=gb: e.matmul(psf[pso][:, 0:257], lhsT=PT[sl][:, mi * 128:(mi + 1) * 128],
                                                            rhs=Va[hb][:, i0b + gb, 0:257], start=(gb == 0), stop=(gb == nblk - 1)),
                     reads=[("PT", sl), ("Va", hb), ("Va", hb, 0), ("Va", hb, 1), ("Va", hb, 2), ("Va", hb, 3)], writes=[("ps", pso)])
            if c["last"]:
                b2_epi(c, it)

        b2_loads(0)
        b2_etexp(0)
        for i in range(NB + 12):
            if i < NB:
                b2_st0(i)
            if 0 <= i - 3 < NB:
                b2_st1(i - 3, i)
            for fn in deferred.pop(i, []):
                fn()
        assert not deferred
        S.barrier()
        if debug:
            S.op("sp", lambda e: e.dma_start(out=oaD, in_=oaT.rearrange("p h t -> p (h t)")), dma=True)
            S.op("sp", lambda e: e.dma_start(out=obD, in_=obT.rearrange("p h t -> p (h t)")), dma=True)
            S.barrier()

        Cc = Arena(arena, cbase, LIM)
        wsa = Cc.f32(8 * 256).rearrange("p (k c) -> p k c", k=8)
        wsb = Cc.f32(8 * 256).rearrange("p (k c) -> p k c", k=8)
        wba = [Cc.bf16(8 * 256).rearrange("p (k c) -> p k c", k=8) for _ in range(2)]
        wbb = [Cc.bf16(8 * 256).rearrange("p (k c) -> p k c", k=8) for _ in range(2)]
        gta = [Cc.bf16(NTOK) for _ in range(2)]
        gtb = [Cc.bf16(NTOK) for _ in range(2)]
        tA = [Cc.f32(512) for _ in range(2)]
        tB = [Cc.f32(512) for _ in range(2)]
        w_a_v = w_a.rearrange("(k p) c -> p k c", p=128)
        w_b_v = w_b.rearrange("(k p) c -> p k c", p=128)
        r = {"p": 0, "t": 0}
        NT = 8
        for i in range(NT + 2):
            if i < NT:
                c0 = i * 256
                S.op("sp", lambda e, c0=c0: e.dma_start(out=wsa, in_=w_a_v[:, :, c0:c0 + 256]), writes=["wsa"], dma=True)
                S.op("sp", lambda e, c0=c0: e.dma_start(out=wsb, in_=w_b_v[:, :, c0:c0 + 256]), writes=["wsb"], dma=True)
                s = i % 2
                cast2(wba[s], wsa, ("wba", s), "wsa", 8)
                cast2(wbb[s], wsb, ("wbb", s), "wsb", 8)
            if 0 <= i - 1 < NT:
                ii = i - 1
                s = ii % 2
                for ch in range(2):
                    cc = 2 * ii + ch
                    gs = cc % 2
                    S.op("sp", lambda e, cc=cc, gs=gs: e.dma_start(out=gta[gs], in_=gT[cc]), writes=[("gta", gs)], dma=True)
                    S.op("sp", lambda e, cc=cc, gs=gs: e.dma_start(out=gtb[gs], in_=gT[16 + cc]), writes=[("gtb", gs)], dma=True)
                    for tg in range(4):
                        pa = 2 * (r["p"] % 2); pb = pa + 1; r["p"] += 1
                        for k in range(8):
                            S.op("pe", lambda e, pa=pa, s=s, k=k, ch=ch, tg=tg: e.matmul(
                                psf[pa], lhsT=wba[s][:, k, ch * 128:(ch + 1) * 128], rhs=oaT[:, k, tg * 512:(tg + 1) * 512],
                                start=(k == 0), stop=(k == 7)), reads=[("wba", s, 0), ("wba", s, 1)], writes=[("ps", pa)])
                        for k in range(8):
                            S.op("pe", lambda e, pb=pb, s=s, k=k, ch=ch, tg=tg: e.matmul(
                                psf[pb], lhsT=wbb[s][:, k, ch * 128:(ch + 1) * 128], rhs=obT[:, k, tg * 512:(tg + 1) * 512],
                                start=(k == 0), stop=(k == 7)), reads=[("wbb", s, 0), ("wbb", s, 1)], writes=[("ps", pb)])
                        t = r["t"] % 2; r["t"] += 1
                        S.op("dve", lambda e, pa=pa, t=t, gs=gs, tg=tg: e.tensor_tensor(
                            out=tA[t], in0=psf[pa], in1=gta[gs][:, tg * 512:(tg + 1) * 512], op=ALU.mult),
                            reads=[("ps", pa), ("gta", gs)], writes=[("tA", t)])
                        S.op("dve", lambda e, pb=pb, t=t, gs=gs, tg=tg: e.tensor_tensor(
                            out=tB[t], in0=psf[pb], in1=gtb[gs][:, tg * 512:(tg + 1) * 512], op=ALU.mult),
                            reads=[("ps", pb), ("gtb", gs)], writes=[("tB", t)])
                        S.op("pool", lambda e, t=t, cc=cc, tg=tg: e.tensor_tensor(
                            out=mT[:, cc, tg * 512:(tg + 1) * 512], in0=tA[t], in1=tB[t], op=ALU.add),
                            reads=[("tA", t), ("tB", t)], writes=[("mT", cc, tg)])
        S.barrier()

        Dd = Arena(arena, smallbase, attbase)
        wso = Dd.f32(16 * 256).rearrange("p (k c) -> p k c", k=16)
        wbo = [Dd.bf16(16 * 256).rearrange("p (k c) -> p k c", k=16) for _ in range(2)]
        xres = [Dd.f32(2 * NTOK).rearrange("p (c t) -> p c t", c=2) for _ in range(2)]
        Dd2 = Arena(arena, cbase, LIM)
        htile = [Dd2.f32(512) for _ in range(4)]
        sqd = [Dd2.f32(512) for _ in range(2)]
        dacc = [Dd2.f32(512) for _ in range(4)]
        w_o_v = w_o.rearrange("(k p) c -> p k c", p=128)
        r = {"p": 0, "h": 0, "q": 0}
        NT = 8
        for i in range(NT + 2):
            if i < NT:
                c0 = i * 256
                s = i % 2
                S.op("sp", lambda e, c0=c0: e.dma_start(out=wso, in_=w_o_v[:, :, c0:c0 + 256]), writes=["wso"], dma=True)
                S.op("sp", lambda e, i=i, s=s: e.dma_start(out=xres[s], in_=xoT_v[:, 2 * i:2 * i + 2, :]),
                     writes=[("xres", s)], dma=True)
                cast2(wbo[s], wso, ("wbo", s), "wso", 16)
            if 0 <= i - 1 < NT:
                ii = i - 1
                s = ii % 2
                for ch in range(2):
                    cc = 2 * ii + ch
                    gs = cc % 2
                    S.op("sp", lambda e, cc=cc, gs=gs: e.dma_start(out=gta[gs], in_=gT[cc]), writes=[("gta", gs)], dma=True)
                    S.op("sp", lambda e, cc=cc, gs=gs: e.dma_start(out=gtb[gs], in_=gT[16 + cc]), writes=[("gtb", gs)], dma=True)
                    for tg in range(4):
                        pa = r["p"] % 4; r["p"] += 1
                        for k in range(16):
                            S.op("pe", lambda e, pa=pa, s=s, k=k, ch=ch, tg=tg: e.matmul(
                                psf[pa], lhsT=wbo[s][:, k, ch * 128:(ch + 1) * 128], rhs=mT[:, k, tg * 512:(tg + 1) * 512],
                                start=(k == 0), stop=(k == 15)), reads=[("wbo", s, 0), ("wbo", s, 1)], writes=[("ps", pa)])
                        hh = r["h"] % 4; r["h"] += 1
                        S.op("dve", lambda e, pa=pa, hh=hh, s=s, ch=ch, tg=tg: e.tensor_tensor(
                            out=htile[hh], in0=psf[pa], in1=xres[s][:, ch, tg * 512:(tg + 1) * 512], op=ALU.add),
                            reads=[("ps", pa), ("xres", s)], writes=[("ht", hh)])
                        S.op("sp", lambda e, hh=hh, cc=cc, tg=tg: e.dma_start(
                            out=hT[cc][:, tg * 512:(tg + 1) * 512], in_=htile[hh]), reads=[("ht", hh)], dma=True)
                        if cc == 0:
                            S.op("act", lambda e, hh=hh, tg=tg: e.activation(out=dacc[tg], in_=htile[hh], func=AF.Square),
                                 reads=[("ht", hh)], writes=[("dacc", tg)])
                        else:
                            q = r["q"] % 2; r["q"] += 1
                            S.op("act", lambda e, hh=hh, q=q: e.activation(out=sqd[q], in_=htile[hh], func=AF.Square),
                                 reads=[("ht", hh)], writes=[("sqd", q)])
                            S.op("pool", lambda e, q=q, tg=tg: e.tensor_tensor(out=dacc[tg], in0=dacc[tg], in1=sqd[q], op=ALU.add),
                                 reads=[("sqd", q), ("dacc", tg)], writes=[("dacc", tg)])
        for tg in range(4):
            S.op("pe", lambda e, tg=tg: e.matmul(psf[4 + tg], lhsT=ones[:, 0:128], rhs=dacc[tg], start=True, stop=True),
                 reads=[("dacc", tg)], writes=[("ps", 4 + tg)])
        for tg in range(4):
            S.op("act", lambda e, tg=tg: e.activation(out=rstd2[:, tg * 512:(tg + 1) * 512], in_=psf[4 + tg],
                                                      func=AF.Ln, scale=1.0 / D, bias=EPS),
                 reads=[("ps", 4 + tg)], writes=[("rstd2", tg)])
            S.op("act", lambda e, tg=tg: e.activation(out=rstd2[:, tg * 512:(tg + 1) * 512],
                                                      in_=rstd2[:, tg * 512:(tg + 1) * 512], func=AF.Exp, scale=-0.5),
                 reads=[("rstd2", tg)], writes=[("rstd2", tg)])
        S.barrier()

        hT_v = hT.rearrange("c p t -> p c t")
        w_g_v = w_g.rearrange("(k p) c -> p k c", p=128)
        w_u_v = w_u.rearrange("(k p) c -> p k c", p=128)
        w_d_v = w_d.rearrange("(k p) c -> p k c", p=128)
        out_ops = []
        for th in range(2):
            t0 = th * 1024
            E = Arena(arena, smallbase, LIM)
            hn = E.bf16(16 * 1024).rearrange("p (k t) -> p k t", k=16)
            hid = E.bf16(NFC * 1024).rearrange("p (f t) -> p f t", f=NFC)
            ebase = E.off
            def norm2(tt, hlb, hn=hn):
                nh = len(hlb)
                for k in range(16):
                    s = k % nh
                    S.op("sp", lambda e, k=k, s=s: e.dma_start(out=hlb[s], in_=hT[k][:, tt:tt + 1024]),
                         writes=[("hl", s)], dma=True)
                    S.op("dve", lambda e, k=k, s=s: e.scalar_tensor_tensor(
                        out=hn[:, k, :], in0=hlb[s], scalar=g2[:, k:k + 1], in1=rstd2[:, tt:tt + 1024],
                        op0=ALU.mult, op1=ALU.mult), reads=[("hl", s)], writes=[("hn", k)])
            if th == 0:
                norm2(0, [E.f32(1024) for _ in range(4)])
                S.barrier()
            E = Arena(arena, ebase, LIM)
            wsg = E.f32(16 * 256).rearrange("p (k c) -> p k c", k=16)
            wsu = E.f32(16 * 256).rearrange("p (k c) -> p k c", k=16)
            wbg = [E.bf16(16 * 256).rearrange("p (k c) -> p k c", k=16) for _ in range(2)]
            wbu = [E.bf16(16 * 256).rearrange("p (k c) -> p k c", k=16) for _ in range(2)]
            sg = [E.f32(512) for _ in range(2)]
            r = {"p": 0, "s": 0}
            NT = NFC // 2
            for i in range(NT + 1):
                if i < NT:
                    c0 = i * 256
                    S.op("sp", lambda e, c0=c0: e.dma_start(out=wsg, in_=w_g_v[:, :, c0:c0 + 256]), writes=["wsg"], dma=True)
                    S.op("sp", lambda e, c0=c0: e.dma_start(out=wsu, in_=w_u_v[:, :, c0:c0 + 256]), writes=["wsu"], dma=True)
                    s = i % 2
                    cast2(wbg[s], wsg, ("wbg", s), "wsg", 16)
                    cast2(wbu[s], wsu, ("wbu", s), "wsu", 16)
                if 0 <= i - 1 < NT:
                    ii = i - 1
                    s = ii % 2
                    for ch in range(2):
                        fc = 2 * ii + ch
                        for tg in range(2):
                            pg = 2 * (r["p"] % 4); pu = pg + 1; r["p"] += 1
                            for k in range(16):
                                S.op("pe", lambda e, pg=pg, s=s, k=k, ch=ch, tg=tg: e.matmul(
                                    psf[pg], lhsT=wbg[s][:, k, ch * 128:(ch + 1) * 128], rhs=hn[:, k, tg * 512:(tg + 1) * 512],
                                    start=(k == 0), stop=(k == 15)), reads=[("wbg", s, 0), ("wbg", s, 1)], writes=[("ps", pg)])
                            for k in range(16):
                                S.op("pe", lambda e, pu=pu, s=s, k=k, ch=ch, tg=tg: e.matmul(
                                    psf[pu], lhsT=wbu[s][:, k, ch * 128:(ch + 1) * 128], rhs=hn[:, k, tg * 512:(tg + 1) * 512],
                                    start=(k == 0), stop=(k == 15)), reads=[("wbu", s, 0), ("wbu", s, 1)], writes=[("ps", pu)])
                            q = r["s"] % 2; r["s"] += 1
                            S.op("act", lambda e, pg=pg, q=q: e.activation(out=sg[q], in_=psf[pg], func=AF.Silu),
                                 reads=[("ps", pg)], writes=[("sg", q)])
                            S.op("dve", lambda e, pu=pu, q=q, fc=fc, tg=tg: e.tensor_tensor(
                                out=hid[:, fc, tg * 512:(tg + 1) * 512], in0=psf[pu], in1=sg[q], op=ALU.mult),
                                reads=[("ps", pu), ("sg", q)], writes=[("hid", fc, tg)])
            S.barrier()
            E = Arena(arena, ebase, LIM)
            wsd = [E.f32(22 * 128).rearrange("p (k c) -> p k c", k=22) for _ in range(2)]
            wbd = [E.bf16(NFC * 128).rearrange("p (k c) -> p k c", k=NFC) for _ in range(2)]
            hres = [E.f32(1024) for _ in range(2)]
            ot = [E.f32(512) for _ in range(4)]
            hl2 = [E.f32(1024) for _ in range(2)]
            r = {"p": 0, "o": 0}
            NT = 16
            for i in range(NT + 2):
                if i == 3 and th == 0:
                    norm2(1024, hl2)
                if i < NT:
                    for hf in range(2):
                        S.op("sp", lambda e, i=i, hf=hf: e.dma_start(
                            out=wsd[hf], in_=w_d_v[:, hf * 22:(hf + 1) * 22, i * 128:(i + 1) * 128]),
                            writes=[("wsd", hf)], dma=True)
                    S.op("sp", lambda e, i=i, t0=t0: e.dma_start(out=hres[i % 2], in_=hT[i][:, t0:t0 + 1024]),
                         writes=[("hres", i % 2)], dma=True)
                    s = i % 2
                    S.op("dve", lambda e, s=s: e.tensor_copy(out=wbd[s][:, 0:22, :], in_=wsd[0]),
                         reads=[("wsd", 0)], writes=[("wbd", s, 0)])
                    S.op("act", lambda e, s=s: e.copy(out=wbd[s][:, 22:44, :], in_=wsd[1]),
                         reads=[("wsd", 1)], writes=[("wbd", s, 1)])
                if 0 <= i - 1 < NT:
                    cc = i - 1
                    s = cc % 2
                    for tg in range(2):
                        pa = r["p"] % 4; r["p"] += 1
                        for f in range(NFC):
                            S.op("pe", lambda e, pa=pa, s=s, f=f, tg=tg: e.matmul(
                                psf[pa], lhsT=wbd[s][:, f, :], rhs=hid[:, f, tg * 512:(tg + 1) * 512],
                                start=(f == 0), stop=(f == NFC - 1)),
                                reads=[("wbd", s, 0), ("wbd", s, 1)], writes=[("ps", pa)])
                        o = r["o"] % 4; r["o"] += 1
                        S.op("dve", lambda e, pa=pa, o=o, s=s, tg=tg: e.tensor_tensor(
                            out=ot[o], in0=psf[pa], in1=hres[s][:, tg * 512:(tg + 1) * 512], op=ALU.add),
                            reads=[("ps", pa), ("hres", s)], writes=[("ot", o)])
                        out_ops.append(S.op("sp", lambda e, o=o, cc=cc, tg=tg, t0=t0: e.dma_start(
                            out=outT[cc * 128:(cc + 1) * 128, t0 + tg * 512:t0 + (tg + 1) * 512], in_=ot[o]),
                            reads=[("ot", o)], dma=True))
            S.barrier()
        S.emit(block)
    return nc


_NC_CACHE = {}


def _host_consts():
    consts = {}
    slopes = np.exp2(-8.0 * (np.arange(4, dtype=np.float32) + 1.0) / 4).astype(np.float32)
    p = np.arange(128)[:, None, None]
    m = np.arange(32)[None, :, None]
    ti = np.arange(128)[None, None, :]
    for c in range(2):
        dist = np.abs(128 * (m + c - 2) + ti + p + 1).astype(np.float32)
        allowed = (2 * (1 - m) + (127 - p) // 64) <= (2 * c + ti // 64)
        eb = np.empty((4, 128, 32, 128), np.float32)
        for h in range(4):
            eb[h] = np.where(allowed, -slopes[h] * dist, np.float32(NEG))
        consts[("ebias", c)] = np.ascontiguousarray(eb.reshape(4, 128, 4096))
        tq = np.arange(128)[:, None]
        u = np.arange(256)[None, :]
        consts[("sbmask", c)] = np.ascontiguousarray((u + tq <= 255 - 128 * c).astype(np.float32))
    return consts


def _prep_inputs(x, norm1_g, w_in, q_norm_g, k_norm_g, lambda_q1, lambda_k1, lambda_q2, lambda_k2,
                 subln_g, w_branch_a, w_branch_b, w_out, norm2_g, w_ffn_gate, w_ffn_up, w_ffn_down):
    f = lambda a: np.ascontiguousarray(np.asarray(a, dtype=np.float32))
    consts = _host_consts()
    shared = {
        "w_in": f(w_in[0]), "w_a": f(w_branch_a[0]), "w_b": f(w_branch_b[0]), "w_o": f(w_out[0]),
        "w_g": f(w_ffn_gate[0]), "w_u": f(w_ffn_up[0]), "w_d": f(w_ffn_down[0]),
        "g1": f(np.asarray(norm1_g[0]).reshape(16, 128).T), "g2": f(np.asarray(norm2_g[0]).reshape(16, 128).T),
        "gq": f(np.asarray(q_norm_g[0]).reshape(128, 1)), "gk": f(np.asarray(k_norm_g[0]).reshape(128, 1)),
        "lamv": f(np.tile(np.concatenate([np.asarray(lambda_q1[0]), np.asarray(lambda_k1[0]),
                                          np.asarray(lambda_q2[0]), np.asarray(lambda_k2[0])])[None, :], (128, 1))),
        "gsub": f(np.tile(np.asarray(subln_g[0])[None, :], (128, 1))),
    }
    x = np.asarray(x, dtype=np.float32)
    in_maps = []
    for core in range(8):
        b, c = core // 2, core % 2
        xb = x[b]
        own = xb.reshape(32, 128, D)[c::2].reshape(NTOK, D)
        m = dict(shared)
        m["xoT"] = np.ascontiguousarray(own.T)
        m["xfT"] = np.ascontiguousarray(xb[::-1].T)
        m["ebias"] = consts[("ebias", c)]
        m["sbmask"] = consts[("sbmask", c)]
        in_maps.append(m)
    return in_maps


def kernel(**inputs):
    in_maps = _prep_inputs(**inputs)
    if "nc" not in _NC_CACHE:
        _NC_CACHE["nc"] = build_program()
    nc = _NC_CACHE["nc"]
    res = run_bass_kernel_spmd(nc, in_maps, core_ids=list(range(8)))
    out = np.empty((4, SEQ, D), np.float32)
    for core in range(8):
        b, c = core // 2, core % 2
        o = np.asarray(res.results[core]["outT"]).T.reshape(16, 128, D)
        out[b].reshape(32, 128, D)[c::2] = o
    return out
```

```python
import contextlib
import math
import numpy as np
import concourse.bass as bass
import concourse.mybir as mybir
from concourse.bass_utils import run_bass_kernel_spmd

F32 = mybir.dt.float32
BF16 = mybir.dt.bfloat16
AF = mybir.ActivationFunctionType
ALU = mybir.AluOpType
AX = mybir.AxisListType

D = 2048
SEQ = 4096
NTOK = 2048
DFF = 5632
NFC = DFF // 128
EPS = 1e-6
SUBLN_EPS = 1e-5
SCALE = 1.0 / math.sqrt(128.0)
LAMBDA_INIT = 0.8 - 0.6 * math.exp(-0.3 * 0)
NEG = -30000.0


class Op:
    __slots__ = ("eng", "fn", "dma", "deps", "idx", "sig", "dsem", "dval", "n")

    def __init__(self, eng, fn, dma):
        self.eng = eng; self.fn = fn; self.dma = dma
        self.deps = (); self.idx = None; self.sig = False
        self.dsem = None; self.dval = None; self.n = None


class Sched:
    KDMA = 8
    ENGS = ("sp", "pe", "act", "dve", "pool")
    QUEUES = ("sp",)

    def __init__(self, nc, stack):
        self.nc = nc
        self.ops = []
        self.kw = {}
        self.kr = {}
        self.esem = {e: stack.enter_context(nc.semaphore("s_" + e)) for e in self.ENGS}
        self.dsems = {q: [stack.enter_context(nc.semaphore(f"d_{q}{i}")) for i in range(self.KDMA)]
                      for q in self.QUEUES}
        self.last_c = {}
        self.last_d = {q: [] for q in self.QUEUES}

    def op(self, eng, fn, reads=(), writes=(), dma=False, deps=()):
        o = Op(eng, fn, dma)
        d = set(deps)
        kw = self.kw; kr = self.kr
        for k in reads:
            w = kw.get(k)
            if w is not None:
                d.add(w)
        for k in writes:
            w = kw.get(k)
            if w is not None:
                d.add(w)
            r = kr.get(k)
            if r:
                d.update(r)
        d.discard(o)
        o.deps = d
        for k in reads:
            kr.setdefault(k, []).append(o)
        for k in writes:
            kw[k] = o
            kr[k] = []
        self.ops.append(o)
        if dma:
            l = self.last_d[eng]
            l.append(o)
            if len(l) > self.KDMA:
                l.pop(0)
        elif fn is not None:
            self.last_c[eng] = o
        return o

    def barrier(self):
        lasts = list(self.last_c.values())
        for q in self.QUEUES:
            lasts.extend(self.last_d[q])
        for e in self.ENGS:
            self.op(e, None, deps=lasts)
        self.kw.clear(); self.kr.clear()

    def emit(self, block):
        for o in self.ops:
            for d in o.deps:
                if d.dma:
                    continue
                if d.eng == "pe" and o.eng == "pe" and not o.dma:
                    continue
                d.sig = True
        cnt = {e: 0 for e in self.ENGS}
        dcnt = {q: 0 for q in self.QUEUES}
        for o in self.ops:
            if o.dma:
                n = dcnt[o.eng]; dcnt[o.eng] += 1
                o.n = n
                o.dsem = self.dsems[o.eng][n % self.KDMA]
                o.dval = 16 * (n // self.KDMA + 1)
            elif o.sig:
                cnt[o.eng] += 1
                o.idx = cnt[o.eng]
        for e in cnt:
            assert cnt[e] < 60000, (e, cnt[e])
        for q in dcnt:
            assert 16 * (dcnt[q] // self.KDMA + 1) < 60000
        self.counts = (cnt, dcnt)
        per = {e: [o for o in self.ops if o.eng == e] for e in self.ENGS}

        def run(ename):
            def body(eng):
                waited = {}

                def wait(sem, val):
                    key = id(sem)
                    if waited.get(key, 0) >= val:
                        return
                    waited[key] = val
                    eng.wait_ge(sem, val)
                for o in per[ename]:
                    for d in o.deps:
                        if d.dma:
                            wait(d.dsem, d.dval)
                        else:
                            if d.eng == "pe" and ename == "pe" and not o.dma:
                                continue
                            wait(self.esem[d.eng], d.idx)
                    if o.dma:
                        if o.n >= self.KDMA:
                            wait(o.dsem, o.dval - 16)
                        o.fn(eng).then_inc(o.dsem, 16)
                    elif o.fn is not None:
                        ins = o.fn(eng)
                        if o.sig:
                            ins.then_inc(self.esem[ename], 1)
            return body
        block.sync(run("sp"))
        block.tensor(run("pe"))
        block.scalar(run("act"))
        block.vector(run("dve"))
        block.gpsimd(run("pool"))


class Arena:
    def __init__(self, ap, base_bytes, limit_bytes):
        self.ap = ap; self.off = base_bytes; self.limit = limit_bytes

    def _take(self, nbytes):
        nbytes = (nbytes + 63) // 64 * 64
        o = self.off
        self.off += nbytes
        assert self.off <= self.limit, ("arena overflow", self.off, self.limit)
        return o

    def f32(self, ncols):
        o = self._take(ncols * 4)
        return self.ap[:, o // 4: o // 4 + ncols]

    def bf16(self, ncols):
        assert ncols % 2 == 0
        o = self._take(ncols * 2)
        return self.ap[:, o // 4: o // 4 + ncols // 2].bitcast(BF16)


ARENA_KB = 206


def build_program(debug=False):
    nc = bass.Bass("TRN2", target_bir_lowering=False)
    dt_in = lambda n, s, d=F32: nc.dram_tensor(n, s, d, kind="ExternalInput").ap()
    xoT = dt_in("xoT", [D, NTOK])
    xfT = dt_in("xfT", [D, SEQ])
    w_in = dt_in("w_in", [D, 10240])
    w_a = dt_in("w_a", [1024, D])
    w_b = dt_in("w_b", [1024, D])
    w_o = dt_in("w_o", [D, D])
    w_g = dt_in("w_g", [D, DFF])
    w_u = dt_in("w_u", [D, DFF])
    w_d = dt_in("w_d", [DFF, D])
    g1_d = dt_in("g1", [128, 16])
    g2_d = dt_in("g2", [128, 16])
    gq_d = dt_in("gq", [128, 1])
    gk_d = dt_in("gk", [128, 1])
    lamv_d = dt_in("lamv", [128, 512])
    gsub_d = dt_in("gsub", [128, 256])
    ebias_d = dt_in("ebias", [4, 128, 4096])
    sbmask_d = dt_in("sbmask", [128, 256])
    outT = nc.dram_tensor("outT", [D, NTOK], F32, kind="ExternalOutput").ap()
    skind = "ExternalOutput" if debug else "Internal"
    qT = nc.dram_tensor("qT", [16, 128, NTOK], BF16, kind=skind).ap()
    kT = nc.dram_tensor("kT", [16, 128, SEQ], BF16, kind=skind).ap()
    vA = nc.dram_tensor("vA", [8, 128, SEQ], BF16, kind=skind).ap()
    vD = nc.dram_tensor("vD", [4, 128, 2 * SEQ], BF16, kind=skind).ap()
    gT = nc.dram_tensor("gT", [32, 128, NTOK], BF16, kind=skind).ap()
    hT = nc.dram_tensor("hT", [16, 128, NTOK], F32, kind=skind).ap()
    oaD = nc.dram_tensor("oaD", [128, 8 * NTOK], BF16, kind=skind).ap() if debug else None
    obD = nc.dram_tensor("obD", [128, 8 * NTOK], BF16, kind=skind).ap() if debug else None

    with contextlib.ExitStack() as st:
        S = Sched(nc, st)
        arena_t = st.enter_context(nc.sbuf_tensor("arena", [128, ARENA_KB * 256], F32))
        arena = arena_t[:]
        LIM = ARENA_KB * 1024
        ps = [st.enter_context(nc.psum_tensor(f"ps{i}", [128, 512], F32)) for i in range(8)]
        psf = [p[:] for p in ps]
        psb = [p[:].bitcast(BF16) for p in ps]
        block = st.enter_context(nc.Block())

        C = Arena(arena, 0, LIM)
        ident = C.bf16(128)
        ones = C.f32(528)
        g1 = C.f32(16); g2 = C.f32(16); gq = C.f32(2); gk = C.f32(2)
        gsub = C.f32(256); sbmask = C.f32(256)
        nlam = C.f32(2)
        rstd2 = C.f32(NTOK)
        maskneg = C.bf16(256)
        smallbase = C.off

        uid = [0]

        def key(n):
            uid[0] += 1
            return (n, uid[0])

        def cast2(dst, src, kd, ks, K):
            h2 = K // 2
            S.op("dve", lambda e: e.tensor_copy(out=dst[:, 0:h2, :], in_=src[:, 0:h2, :]), reads=[ks], writes=[kd + (0,)])
            S.op("act", lambda e: e.copy(out=dst[:, h2:K, :], in_=src[:, h2:K, :]), reads=[ks], writes=[kd + (1,)])

        def load(dst, src, k, q="sp"):
            return S.op(q, lambda e: e.dma_start(out=dst, in_=src), writes=[k], dma=True)

        load(g1, g1_d, "g1"); load(g2, g2_d, "g2")
        load(gq[:, 0:1], gq_d, "gq"); load(gk[:, 0:1], gk_d, "gk")
        load(gsub, gsub_d, "gsub"); load(sbmask, sbmask_d, "sbmask")
        S.op("pool", lambda e: e.memset(ident, 0.0), writes=["ident"])
        S.op("pool", lambda e: e.affine_select(out=ident, in_=ident, pattern=[[-1, 128]],
                                               compare_op=ALU.not_equal, fill=1.0, base=0,
                                               channel_multiplier=1), reads=["ident"], writes=["ident"])
        S.op("dve", lambda e: e.memset(ones, 1.0), writes=["ones"])
        S.op("dve", lambda e: e.tensor_scalar_mul(out=maskneg, in0=sbmask, scalar1=NEG), reads=["sbmask"], writes=["maskneg"])
        S.op("dve", lambda e: e.tensor_scalar_mul(out=gsub, in0=gsub, scalar1=1.0 - LAMBDA_INIT),
             reads=["gsub"], writes=["gsub"])
        T = Arena(arena, smallbase, LIM)
        lamv = T.f32(512); lt = T.f32(256); ls = T.f32(4)
        load(lamv, lamv_d, "lamv")
        for i in range(2):
            S.op("dve", lambda e, i=i: e.tensor_tensor(out=lt[:, i * 128:(i + 1) * 128],
                                                       in0=lamv[:, i * 256:i * 256 + 128],
                                                       in1=lamv[:, i * 256 + 128:i * 256 + 256], op=ALU.mult),
                 reads=["lamv"], writes=[("lt", i)])
            S.op("dve", lambda e, i=i: e.reduce_sum(out=ls[:, i:i + 1], in_=lt[:, i * 128:(i + 1) * 128], axis=AX.X),
                 reads=[("lt", i)], writes=[("ls", i)])
            S.op("act", lambda e, i=i: e.activation(out=ls[:, 2 + i:3 + i], in_=ls[:, i:i + 1], func=AF.Exp),
                 reads=[("ls", i)], writes=[("le", i)])
        S.op("dve", lambda e: e.tensor_tensor(out=nlam[:, 0:1], in0=ls[:, 3:4], in1=ls[:, 2:3], op=ALU.subtract),
             reads=[("le", 0), ("le", 1)], writes=["nlam"])
        S.op("dve", lambda e: e.tensor_scalar_add(out=nlam[:, 0:1], in0=nlam[:, 0:1], scalar1=-LAMBDA_INIT),
             reads=["nlam"], writes=["nlam"])
        S.barrier()

        A = Arena(arena, smallbase, LIM)
        XTOP = LIM - 64 * 1024
        AT = Arena(arena, XTOP, LIM)
        xn = [AT.bf16(16 * 1024).rearrange("p (k t) -> p k t", k=16) for _ in range(2)]
        A.limit = XTOP
        xs = A.f32(16 * 512).rearrange("p (k t) -> p k t", k=16)
        sq = [A.f32(512) for _ in range(2)]
        rs = A.f32(512)
        nacc = A.f32(512)
        wst = [A.f32(16 * 256).rearrange("p (k c) -> p k c", k=16) for _ in range(2)]
        wbf = [A.bf16(16 * 256).rearrange("p (k c) -> p k c", k=16) for _ in range(2)]
        sq2 = [A.f32(512) for _ in range(2)]
        rs2 = [A.f32(512) for _ in range(2)]
        obuf = [A.bf16(512) for _ in range(4)]
        w_in_v = w_in.rearrange("(k p) c -> p k c", p=128)
        xoT_v = xoT.rearrange("(k p) t -> p k t", p=128)
        xfT_v = xfT.rearrange("(k p) t -> p k t", p=128)

        rot = {"psA": 0, "ob": 0, "st2": 0, "w": 0}

        def norm_dma(src_v, c0, tg):
            S.op("sp", lambda e: e.dma_start(out=xs, in_=src_v[:, :, c0 + tg * 512: c0 + (tg + 1) * 512]),
                 writes=["xs"], dma=True)

        def norm_sq():
            S.op("pool", lambda e: e.tensor_tensor(out=nacc, in0=xs[:, 0, :], in1=xs[:, 0, :], op=ALU.mult),
                 reads=["xs"], writes=["nacc"])
            for k in range(1, 16):
                S.op("pool", lambda e, k=k: e.tensor_tensor(out=sq[0], in0=xs[:, k, :], in1=xs[:, k, :], op=ALU.mult),
                     reads=["xs"], writes=[("sq", 0)])
                S.op("pool", lambda e, k=k: e.tensor_tensor(out=nacc, in0=nacc, in1=sq[0], op=ALU.add),
                     reads=[("sq", 0), "nacc"], writes=["nacc"])

        def norm_part2(pi, tg):
            pp = pi % 2
            S.op("pe", lambda e: e.matmul(psf[4], lhsT=ones[:, 0:128], rhs=nacc, start=True, stop=True),
                 reads=["nacc"], writes=["ps4"])
            S.op("act", lambda e: e.activation(out=rs, in_=psf[4], func=AF.Ln, scale=1.0 / D, bias=EPS),
                 reads=["ps4"], writes=["rs"])
            S.op("act", lambda e: e.activation(out=rs, in_=rs, func=AF.Exp, scale=-0.5),
                 reads=["rs"], writes=["rs"])
            for k in range(16):
                S.op("dve", lambda e, k=k: e.scalar_tensor_tensor(
                    out=xn[pp][:, k, tg * 512:(tg + 1) * 512], in0=xs[:, k, :], scalar=g1[:, k:k + 1],
                    in1=rs, op0=ALU.mult, op1=ALU.mult),
                    reads=["xs", "rs", "g1"], writes=[("xn", pp, tg, k)])

        pending = []

        def flush():
            for f in pending:
                f()
            del pending[:]

        def qknorm_evac(psi, gcol, ob, kps, dst):
            s2 = rot["st2"] % 2; rot["st2"] += 1
            stp = 5 + s2
            S.op("act", lambda e: e.activation(out=sq2[s2], in_=psf[psi], func=AF.Square),
                 reads=[kps], writes=[("sq2", s2)])

            def tail():
                S.op("pe", lambda e: e.matmul(psf[stp], lhsT=ones[:, 0:128], rhs=sq2[s2], start=True, stop=True),
                     reads=[("sq2", s2), "ones"], writes=[("ps", stp)])
                S.op("act", lambda e: e.activation(out=rs2[s2], in_=psf[stp], func=AF.Ln, scale=1.0 / 128, bias=EPS),
                     reads=[("ps", stp)], writes=[("rs2", s2)])
                S.op("act", lambda e: e.activation(out=rs2[s2], in_=rs2[s2], func=AF.Exp, scale=-0.5),
                     reads=[("rs2", s2)], writes=[("rs2", s2)])
                S.op("dve", lambda e: e.scalar_tensor_tensor(out=obuf[ob], in0=psf[psi], scalar=gcol, in1=rs2[s2],
                                                             op0=ALU.mult, op1=ALU.mult),
                     reads=[kps, ("rs2", s2)], writes=[("ob", ob)])
                S.op("sp", lambda e: e.dma_start(out=dst, in_=obuf[ob]), reads=[("ob", ob)], dma=True)
            pending.append(tail)

        def compute_tile(pi, tile, tok0, s):
            pp = pi % 2
            if True:
                if True:
                    c0, kind, ch0, _dst = tile
                    if kind == "v":
                        for tb in range(8):
                            psi = rot["psA"] % 4; rot["psA"] += 1
                            for k in range(16):
                                S.op("pe", lambda e, k=k, tb=tb, psi=psi, s=s: e.matmul(
                                    psf[psi][:, 0:256], lhsT=xn[pp][:, k, tb * 128:(tb + 1) * 128], rhs=wbf[s][:, k, :],
                                    start=(k == 0), stop=(k == 15)),
                                    reads=[("wbf", s, 0), ("wbf", s, 1), ("xn", pp, tb // 4, k)], writes=[("ps", psi)])
                            flush()
                            ob = rot["ob"] % 4; rot["ob"] += 1
                            S.op("dve", lambda e, psi=psi, ob=ob: e.tensor_copy(out=obuf[ob][:, 0:256], in_=psf[psi][:, 0:256]),
                                 reads=[("ps", psi)], writes=[("ob", ob)])
                            blk = (tok0 + tb * 128) // 128
                            if ch0 < 1024:
                                h0 = ch0 // 128
                                dstv = vA[h0:h0 + 2].rearrange("h p c -> p h c")[:, :, blk * 128:(blk + 1) * 128]
                                srcv = obuf[ob][:, 0:256].rearrange("p (h c) -> p h c", h=2)
                            else:
                                hd = (ch0 - 1024) // 256
                                dstv = vD[hd][:, blk * 256:(blk + 1) * 256]
                                srcv = obuf[ob][:, 0:256]
                            S.op("sp", lambda e, dstv=dstv, srcv=srcv: e.dma_start(out=dstv, in_=srcv),
                                 reads=[("ob", ob)], dma=True)
                        return
                    for ch in range(2):
                        for tg in range(2):
                            psi = rot["psA"] % 4; rot["psA"] += 1
                            for k in range(16):
                                S.op("pe", lambda e, k=k, ch=ch, tg=tg, psi=psi, s=s: e.matmul(
                                    psf[psi], lhsT=wbf[s][:, k, ch * 128:(ch + 1) * 128],
                                    rhs=xn[pp][:, k, tg * 512:(tg + 1) * 512], start=(k == 0), stop=(k == 15)),
                                    reads=[("wbf", s, 0), ("wbf", s, 1), ("xn", pp, tg, k)], writes=[("ps", psi)])
                            flush()
                            ob = rot["ob"] % 4; rot["ob"] += 1
                            kps = ("ps", psi)
                            dst = _dst[ch0 + ch][:, tok0 + tg * 512: tok0 + (tg + 1) * 512]
                            if kind == "sb":
                                S.op("dve", lambda e, psi=psi, ob=ob: e.tensor_copy(out=obuf[ob], in_=psf[psi]),
                                     reads=[kps], writes=[("ob", ob)])
                            elif kind == "gate":
                                S.op("act", lambda e, psi=psi, ob=ob: e.activation(out=obuf[ob], in_=psf[psi], func=AF.Sigmoid),
                                     reads=[kps], writes=[("ob", ob)])
                            elif kind == "dfq":
                                qknorm_evac(psi, gq[:, 0:1], ob, kps, dst)
                                continue
                            elif kind == "dfk":
                                qknorm_evac(psi, gk[:, 0:1], ob, kps, dst)
                                continue
                            S.op("sp", lambda e, ob=ob, dst=dst: e.dma_start(out=dst, in_=obuf[ob]),
                                 reads=[("ob", ob)], dma=True)


        own_tiles = []
        for t in range(4):
            own_tiles.append((t * 256, "sb", 2 * t, qT))
        for t in range(4):
            own_tiles.append((3072 + t * 256, "dfq", 8 + 2 * t, qT))
        kv_tiles = []
        for t in range(4):
            kv_tiles.append((1024 + t * 256, "sb", 2 * t, kT))
        for t in range(4):
            kv_tiles.append((4096 + t * 256, "dfk", 8 + 2 * t, kT))
        for t in range(4):
            kv_tiles.append((2048 + t * 256, "v", t * 256, None))
        for t in range(4):
            kv_tiles.append((5120 + t * 256, "v", 1024 + t * 256, None))

        passes = [(xfT_v, i * 1024, kv_tiles) for i in range(4)] + \
                 [(xoT_v, 0, own_tiles), (xoT_v, 1024, own_tiles)]
        def norm_first(src_v, c0, tg, pp=0):
            for k in range(16):
                S.op("sp", lambda e, k=k: e.dma_start(out=xs[:, k, :], in_=src_v[:, k, c0 + tg * 512: c0 + (tg + 1) * 512]),
                     writes=[("xs1", k)] + (["xs"] if k == 0 else []), reads=([] if k == 0 else ["xs"]), dma=True)
            for k in range(16):
                S.op("act", lambda e, k=k: e.activation(out=sq[k % 2], in_=xs[:, k, :], func=AF.Square),
                     reads=[("xs1", k)], writes=[("sq", k % 2)])
                S.op("pe", lambda e, k=k: e.matmul(psf[4], lhsT=ones[:, 0:128], rhs=sq[k % 2], start=(k == 0), stop=(k == 15)),
                     reads=[("sq", k % 2)], writes=["ps4"])
            S.op("act", lambda e: e.activation(out=rs, in_=psf[4], func=AF.Ln, scale=1.0 / D, bias=EPS),
                 reads=["ps4"], writes=["rs"])
            S.op("act", lambda e: e.activation(out=rs, in_=rs, func=AF.Exp, scale=-0.5),
                 reads=["rs"], writes=["rs"])
            for k in range(16):
                S.op("dve", lambda e, k=k: e.scalar_tensor_tensor(
                    out=xn[pp][:, k, tg * 512:(tg + 1) * 512], in0=xs[:, k, :], scalar=g1[:, k:k + 1],
                    in1=rs, op0=ALU.mult, op1=ALU.mult),
                    reads=[("xs1", k), "rs", "g1"], writes=[("xn", pp, tg, k), "xs"])

        for tg in range(2):
            norm_first(passes[0][0], passes[0][1], tg)
        G = []
        for pi, (src_v, c0, tiles) in enumerate(passes):
            hooks = {}
            if pi + 1 < len(passes) and len(tiles) < 14:
                nsrc, nc0, _ = passes[pi + 1]
                hooks[0] = [lambda nsrc=nsrc, nc0=nc0, pi=pi: norm_first(nsrc, nc0, 0, (pi + 1) % 2)]
                hooks[4] = [lambda nsrc=nsrc, nc0=nc0, pi=pi: norm_first(nsrc, nc0, 1, (pi + 1) % 2)]
            elif pi + 1 < len(passes):
                nsrc, nc0, _ = passes[pi + 1]
                hooks[0] = [lambda nsrc=nsrc, nc0=nc0: norm_dma(nsrc, nc0, 0)]
                hooks[2] = [norm_sq]
                hooks[5] = [lambda pi=pi: norm_part2(pi + 1, 0)]
                hooks[8] = [lambda nsrc=nsrc, nc0=nc0: norm_dma(nsrc, nc0, 1)]
                hooks[10] = [norm_sq]
                hooks[13] = [lambda pi=pi: norm_part2(pi + 1, 1)]
            for li, tile in enumerate(tiles):
                G.append((pi, li, tile, c0, hooks))
        NG = len(G)
        for i in range(NG + 2):
            if i < NG:
                c0w = G[i][2][0]
                s = i % 2
                S.op("sp", lambda e, s=s, c0w=c0w: e.dma_start(out=wst[s], in_=w_in_v[:, :, c0w:c0w + 256]),
                     writes=[("wst", s)], dma=True)
            if 0 <= i - 1 < NG:
                s = (i - 1) % 2
                cast2(wbf[s], wst[s], ("wbf", s), ("wst", s), 16)
            if 0 <= i - 2 < NG:
                pi, li, tile, tok0, hooks = G[i - 2]
                for hk in hooks.get(li, ()):
                    hk()
                compute_tile(pi, tile, tok0, (i - 2) % 2)
        flush()
        S.barrier()

        P = Arena(arena, smallbase, LIM)
        oaT = P.bf16(8 * NTOK).rearrange("p (h t) -> p h t", h=8)
        b1base = P.off
        obT = P.bf16(8 * NTOK).rearrange("p (h t) -> p h t", h=8)
        attbase = P.off
        mT = P.bf16(16 * NTOK).rearrange("p (k t) -> p k t", k=16)
        cbase = P.off
        B = Arena(arena, b1base, XTOP)
        kTh = [B.bf16(SEQ) for _ in range(2)]
        vh = [B.bf16(32 * 128).rearrange("p (b d) -> p b d", b=32) for _ in range(2)]
        qTh = [B.bf16(NTOK) for _ in range(2)]
        RB = 5
        OMr = [B.f32(528) for _ in range(RB)]
        PIr = [B.f32(528) for _ in range(RB)]
        Wr = [B.bf16(512) for _ in range(RB)]
        WTr = [B.bf16(512).rearrange("p (b q) -> p b q", b=4) for _ in range(RB)]
        gwst = B.f32(16 * 128).rearrange("p (k c) -> p k c", k=16)
        gwbf = [B.bf16(16 * 128).rearrange("p (k c) -> p k c", k=16) for _ in range(2)]
        gob = [B.bf16(512) for _ in range(4)]

        def b1_loads(h):
            hb = h % 2
            S.op("sp", lambda e: e.dma_start(out=kTh[hb], in_=kT[h]), writes=[("kTh", hb)], dma=True)
            S.op("sp", lambda e: e.dma_start(out=qTh[hb], in_=qT[h]), writes=[("qTh", hb)], dma=True)
            S.op("sp", lambda e: e.dma_start(out=vh[hb], in_=vA[h].rearrange("p (b d) -> p b d", b=32)),
                 writes=[("vh", hb)], dma=True)

        chunks = []
        rowi = 0
        for h in range(8):
            li = 0
            for j in range(16):
                L = 256 * (j + 1)
                offs = list(range(0, L, 512))
                for ci, off in enumerate(offs):
                    chunks.append(dict(h=h, j=j, off=off, w=min(512, L - off), i0=SEQ - L, first=(ci == 0),
                                       last=(ci == len(offs) - 1), nblk=L // 128, pso=4 + rowi % 2, li=li))
                    li += 1
                rowi += 1
        NCH = len(chunks)

        def b1_st0(n):
            c = chunks[n]; sl = n % RB; pss = n % 2
            h, j, off, w, i0 = c["h"], c["j"], c["off"], c["w"], c["i0"]
            hb = h % 2
            if c["li"] == 8 and h + 1 < 8:
                b1_loads(h + 1)
            fst = c["first"]
            S.op("pe", lambda e: e.matmul(psf[pss][:, 0:w], lhsT=qTh[hb][:, j * 128:(j + 1) * 128],
                                          rhs=kTh[hb][:, i0 + off:i0 + off + w], start=True, stop=(not fst)),
                 reads=[("qTh", hb), ("kTh", hb)], writes=[("ps", pss)])
            if fst:
                S.op("pe", lambda e: e.matmul(psf[pss][:, 0:256], lhsT=ident, rhs=maskneg, start=False, stop=True),
                     writes=[("ps", pss)])
            S.op("act", lambda e: e.activation(out=OMr[sl][:, 1:1 + w], in_=psf[pss][:, 0:w], func=AF.Sigmoid, scale=-SCALE),
                 reads=[("ps", pss)], writes=[("OM", sl)])
            if c["first"]:
                S.op("dve", lambda e: e.tensor_tensor_scan(out=PIr[sl][:, 0:w + 1], data0=OMr[sl][:, 0:w + 1],
                                                           data1=ones[:, 0:1].to_broadcast([128, w + 1]), initial=1.0, op0=ALU.mult, op1=ALU.mult),
                     reads=[("OM", sl)], writes=[("PI", sl)])
            else:
                psl = (n - 1) % RB
                S.op("dve", lambda e: e.tensor_tensor_scan(out=PIr[sl][:, 0:w + 1], data0=OMr[sl][:, 0:w + 1],
                                                           data1=ones[:, 0:1].to_broadcast([128, w + 1]), initial=PIr[psl][:, 512:513],
                                                           op0=ALU.mult, op1=ALU.mult),
                     reads=[("OM", sl), ("PI", psl)], writes=[("PI", sl)])
            S.op("pool", lambda e: e.tensor_tensor(out=Wr[sl][:, 0:w], in0=PIr[sl][:, 0:w], in1=PIr[sl][:, 1:1 + w],
                                                   op=ALU.subtract), reads=[("PI", sl)], writes=[("W", sl)])

        def b1_st1(n):
            c = chunks[n]; sl = n % RB; pst = 2 + n % 2
            w = c["w"]; nb = w // 128
            for bi in range(nb):
                S.op("pe", lambda e, bi=bi: e.transpose(psb[pst][:, bi * 128:(bi + 1) * 128],
                                                        Wr[sl][:, bi * 128:(bi + 1) * 128], ident),
                     reads=[("W", sl)], writes=[("ps", pst)])
            S.op("act", lambda e: e.copy(out=WTr[sl][:, 0:nb, :],
                                         in_=psb[pst][:, 0:w].rearrange("p (b q) -> p b q", b=nb)),
                 reads=[("ps", pst)], writes=[("WT", sl)])

        def b1_st2(n):
            c = chunks[n]; sl = n % RB
            h, j, off, w, i0, pso, nblk = c["h"], c["j"], c["off"], c["w"], c["i0"], c["pso"], c["nblk"]
            hb = h % 2; nb = w // 128; b0 = off // 128
            for bi in range(nb):
                gb = b0 + bi
                S.op("pe", lambda e, bi=bi, gb=gb: e.matmul(psf[pso][:, 0:128], lhsT=vh[hb][:, i0 // 128 + gb, :],
                                                            rhs=WTr[sl][:, bi, :], start=(gb == 0), stop=(gb == nblk - 1)),
                     reads=[("vh", hb), ("WT", sl)], writes=[("ps", pso)])
            if c["last"]:
                S.op("act", lambda e: e.copy(out=oaT[:, h, j * 128:(j + 1) * 128], in_=psf[pso][:, 0:128]),
                     reads=[("ps", pso)], writes=[("oaT", h, j)])

        for i in range(RB):
            S.op("pool", lambda e, i=i: e.memset(OMr[i][:, 0:1], 1.0), writes=[("OM", i)])
        gev = []
        gcnt = {"ob": 0}

        def g_load(g):
            c0 = 6144 + (g % 32) * 128
            S.op("sp", lambda e: e.dma_start(out=gwst, in_=w_in_v[:, :, c0:c0 + 128]), writes=["gwst"], dma=True)

        def g_cast(g):
            s_ = g % 2
            S.op("act", lambda e: e.copy(out=gwbf[s_], in_=gwst), reads=["gwst"], writes=[("gwbf", s_)])

        def g_piece(g, tg, pc):
            half = g // 32; t = g % 32; s_ = g % 2
            gb = 6 + tg
            for k in range(4 * pc, 4 * pc + 4):
                S.op("pe", lambda e, k=k: e.matmul(psf[gb], lhsT=gwbf[s_][:, k, :], rhs=xn[half][:, k, tg * 512:(tg + 1) * 512],
                                                   start=(k == 0), stop=(k == 15)),
                     reads=[("gwbf", s_)], writes=[("ps", gb)])
            if pc == 3:
                ob = gcnt["ob"] % 4; gcnt["ob"] += 1
                S.op("act", lambda e: e.activation(out=gob[ob], in_=psf[gb], func=AF.Sigmoid), reads=[("ps", gb)], writes=[("gob", ob)])
                c_lo = half * 1024 + tg * 512
                S.op("sp", lambda e: e.dma_start(out=gT[t][:, c_lo:c_lo + 512], in_=gob[ob]), reads=[("gob", ob)], dma=True)

        NGT = 64
        gev.append([lambda: g_load(0), lambda: g_cast(0), lambda: g_load(1)])
        for g in range(NGT):
            for tg in range(2):
                for pc in range(4):
                    gev.append([lambda g=g, tg=tg, pc=pc: g_piece(g, tg, pc)])
                    if tg == 0 and pc == 1 and g + 1 < NGT:
                        gev.append([lambda g=g: g_cast(g + 1)])
                    if tg == 0 and pc == 3 and g + 2 < NGT:
                        gev.append([lambda g=g: g_load(g + 2)])
        gdone = [0]

        def g_pump(n):
            span = NCH - 40
            tgt = len(gev) if n >= span else min(len(gev), 1 + ((n + 1) * (len(gev) - 1)) // span)
            while gdone[0] < tgt:
                for f in gev[gdone[0]]:
                    f()
                gdone[0] += 1

        b1_loads(0)
        for i in range(NCH + 4):
            if i < NCH:
                g_pump(i)
                b1_st0(i)
            if 0 <= i - 3 < NCH:
                b1_st1(i - 3)
            if 0 <= i - 4 < NCH:
                b1_st2(i - 4)
        S.barrier()

        B = Arena(arena, attbase, LIM)
        kaT = [[B.bf16(SEQ) for _ in range(2)] for _ in range(2)]
        qaT = [[B.bf16(NTOK) for _ in range(2)] for _ in range(2)]
        Va = [B.bf16(32 * 258).rearrange("p (b d) -> p b d", b=32) for _ in range(2)]
        Et = [B.f32(4096) for _ in range(2)]
        RX = 4
        ex = [B.f32(512) for _ in range(2)]
        PT = [B.bf16(512) for _ in range(RX)]
        d1 = [B.f32(256) for _ in range(2)]
        dd = [B.f32(256) for _ in range(4)]
        ob16 = [B.bf16(256) for _ in range(4)]
        sm = [B.f32(8) for _ in range(4)]
        for i in range(2):
            S.op("pool", lambda e, i=i: e.memset(Va[i], 1.0), writes=[("Va", i)])

        def b2_loads(h):
            hb = h % 2
            for a in range(2):
                S.op("sp", lambda e, a=a: e.dma_start(out=kaT[hb][a], in_=kT[8 + 2 * h + a]), writes=[("kaT", hb, a)], dma=True)
                S.op("sp", lambda e, a=a: e.dma_start(out=qaT[hb][a], in_=qT[8 + 2 * h + a]), writes=[("qaT", hb, a)], dma=True)
            vDv = vD[h].rearrange("p (b d) -> p b d", b=32)
            for q4 in range(4):
                S.op("sp", lambda e, q4=q4: e.dma_start(out=Va[hb][:, q4 * 8:(q4 + 1) * 8, 0:256], in_=vDv[:, q4 * 8:(q4 + 1) * 8, :]),
                     writes=[("Va", hb, q4)], reads=[("Va", hb)], dma=True)
            S.op("sp", lambda e: e.dma_start(out=Et[hb], in_=ebias_d[h]), writes=[("Et", hb)], dma=True)

        def b2_etexp(h):
            hb = h % 2
            for c8 in range(8):
                S.op("act", lambda e, c8=c8: e.activation(out=Et[hb][:, c8 * 512:(c8 + 1) * 512],
                                                          in_=Et[hb][:, c8 * 512:(c8 + 1) * 512], func=AF.Exp),
                     reads=[("Et", hb)], writes=[("Et", hb)])

        bats = []
        rowi = 0
        for h in range(4):
            li = 0
            for j in range(16):
                nblk = 2 * j + 2
                for a in range(2):
                    m0s = list(range(0, nblk, 4))
                    for m0 in m0s:
                        bats.append(dict(h=h, j=j, a=a, m0=m0, nb=min(4, nblk - m0), nblk=nblk, eb=rowi % 2,
                                         last=(a == 1 and m0 == m0s[-1]), li=li, r4=rowi % 4))
                        li += 1
                rowi += 1
        NB = len(bats)
        deferred = {}

        def b2_st0(n):
            c = bats[n]; sl = n % RX; pss = n % 2
            h, j, a, m0, nb, nblk = c["h"], c["j"], c["a"], c["m0"], c["nb"], c["nblk"]
            hb = h % 2; i0b = 32 - nblk
            if c["li"] == 12 and h + 1 < 4:
                b2_loads(h + 1)
            if c["li"] == 40 and h + 1 < 4:
                b2_etexp(h + 1)
            for mi in range(nb):
                cb = (i0b + m0 + mi) * 128
                S.op("pe", lambda e, mi=mi, cb=cb: e.matmul(psf[pss][:, mi * 128:(mi + 1) * 128], lhsT=kaT[hb][a][:, cb:cb + 128],
                                                            rhs=qaT[hb][a][:, j * 128:(j + 1) * 128], start=True, stop=True),
                     reads=[("kaT", hb, a), ("qaT", hb, a)], writes=[("ps", pss)])
            S.op("act", lambda e: e.activation(out=ex[pss][:, 0:nb * 128], in_=psf[pss][:, 0:nb * 128], func=AF.Exp, scale=SCALE),
                 reads=[("ps", pss)], writes=[("ex", pss)])
            S.op("dve", lambda e: e.tensor_tensor(out=PT[sl][:, 0:nb * 128], in0=ex[pss][:, 0:nb * 128],
                                                  in1=Et[hb][:, m0 * 128:(m0 + nb) * 128], op=ALU.mult),
                 reads=[("ex", pss), ("Et", hb)], writes=[("PT", sl)])

        def b2_epi(c, it):
            h, j, eb, r4 = c["h"], c["j"], c["eb"], c["r4"]
            p1 = 4 + 2 * eb; p2 = p1 + 1
            smv = sm[r4]
            S.op("dve", lambda e: e.reciprocal(out=smv[:, 0:1], in_=psf[p1][:, 256:257]),
                 reads=[("ps", p1)], writes=[("sm", r4, 0)])
            S.op("dve", lambda e: e.reciprocal(out=smv[:, 1:2], in_=psf[p2][:, 256:257]),
                 reads=[("ps", p2)], writes=[("sm", r4, 1)])
            S.op("dve", lambda e: e.tensor_tensor(out=smv[:, 2:3], in0=smv[:, 1:2], in1=nlam[:, 0:1], op=ALU.mult),
                 reads=[("sm", r4, 1)], writes=[("sm", r4, 2)])
            S.op("dve", lambda e: e.tensor_scalar_mul(out=d1[eb], in0=psf[p1][:, 0:256], scalar1=smv[:, 0:1]),
                 reads=[("ps", p1), ("sm", r4, 0)], writes=[("d1", eb)])
            S.op("dve", lambda e: e.scalar_tensor_tensor(out=dd[r4], in0=psf[p2][:, 0:256], scalar=smv[:, 2:3], in1=d1[eb],
                                                         op0=ALU.mult, op1=ALU.add),
                 reads=[("ps", p2), ("sm", r4, 2), ("d1", eb)], writes=[("dd", r4)])

            def e2():
                S.op("act", lambda e: e.activation(out=d1[eb], in_=dd[r4], func=AF.Square, accum_out=smv[:, 3:4]),
                     reads=[("dd", r4)], writes=[("d1", eb), ("sm", r4, 3)])
                S.op("act", lambda e: e.activation(out=smv[:, 4:5], in_=smv[:, 3:4], func=AF.Ln, scale=1.0 / 256, bias=SUBLN_EPS),
                     reads=[("sm", r4, 3)], writes=[("sm", r4, 4)])
                S.op("act", lambda e: e.activation(out=smv[:, 5:6], in_=smv[:, 4:5], func=AF.Exp, scale=-0.5),
                     reads=[("sm", r4, 4)], writes=[("sm", r4, 5)])

            def e3():
                S.op("dve", lambda e: e.scalar_tensor_tensor(out=ob16[r4], in0=dd[r4], scalar=smv[:, 5:6], in1=gsub,
                                                             op0=ALU.mult, op1=ALU.mult),
                     reads=[("dd", r4), ("sm", r4, 5)], writes=[("ob16", r4)])

            def e4():
                pst = 2 + eb
                for hf in range(2):
                    S.op("pe", lambda e, hf=hf: e.transpose(psb[pst][:, hf * 128:(hf + 1) * 128],
                                                            ob16[r4][:, hf * 128:(hf + 1) * 128], ident),
                         reads=[("ob16", r4)], writes=[("ps", pst)])
                S.op("act", lambda e: e.copy(out=obT[:, 2 * h:2 * h + 2, j * 128:(j + 1) * 128],
                                             in_=psb[pst][:, 0:256].rearrange("p (b q) -> p b q", b=2)),
                     reads=[("ps", pst)], writes=[("obT", h, j)])
            deferred.setdefault(it + 2, []).append(e2)
            deferred.setdefault(it + 4, []).append(e3)
            deferred.setdefault(it + 6, []).append(e4)

        def b2_st1(n, it):
            c = bats[n]; sl = n % RX
            h, a, m0, nb, nblk, eb = c["h"], c["a"], c["m0"], c["nb"], c["nblk"], c["eb"]
            hb = h % 2; i0b = 32 - nblk
            pso = 4 + 2 * eb + a
            for mi in range(nb):
                gb = m0 + mi
                S.op("pe", lambda e, mi=mi, gb=gb: e.matmul(psf[pso][:, 0:257], lhsT=PT[sl][:, mi * 128:(mi + 1) * 128],
                                                            rhs=Va[hb][:, i0b + gb, 0:257], start=(gb == 0), stop=(gb == nblk - 1)),
                     reads=[("PT", sl), ("Va", hb), ("Va", hb, 0), ("Va", hb, 1), ("Va", hb, 2), ("Va", hb, 3)], writes=[("ps", pso)])
            if c["last"]:
                b2_epi(c, it)

        b2_loads(0)
        b2_etexp(0)
        for i in range(NB + 12):
            if i < NB:
                b2_st0(i)
            if 0 <= i - 3 < NB:
                b2_st1(i - 3, i)
            for fn in deferred.pop(i, []):
                fn()
        assert not deferred
        S.barrier()
        if debug:
            S.op("sp", lambda e: e.dma_start(out=oaD, in_=oaT.rearrange("p h t -> p (h t)")), dma=True)
            S.op("sp", lambda e: e.dma_start(out=obD, in_=obT.rearrange("p h t -> p (h t)")), dma=True)
            S.barrier()

        Cc = Arena(arena, cbase, LIM)
        wsa = Cc.f32(8 * 256).rearrange("p (k c) -> p k c", k=8)
        wsb = Cc.f32(8 * 256).rearrange("p (k c) -> p k c", k=8)
        wba = [Cc.bf16(8 * 256).rearrange("p (k c) -> p k c", k=8) for _ in range(2)]
        wbb = [Cc.bf16(8 * 256).rearrange("p (k c) -> p k c", k=8) for _ in range(2)]
        gta = [Cc.bf16(NTOK) for _ in range(2)]
        gtb = [Cc.bf16(NTOK) for _ in range(2)]
        tA = [Cc.f32(512) for _ in range(2)]
        tB = [Cc.f32(512) for _ in range(2)]
        w_a_v = w_a.rearrange("(k p) c -> p k c", p=128)
        w_b_v = w_b.rearrange("(k p) c -> p k c", p=128)
        r = {"p": 0, "t": 0}
        NT = 8
        for i in range(NT + 2):
            if i < NT:
                c0 = i * 256
                S.op("sp", lambda e, c0=c0: e.dma_start(out=wsa, in_=w_a_v[:, :, c0:c0 + 256]), writes=["wsa"], dma=True)
                S.op("sp", lambda e, c0=c0: e.dma_start(out=wsb, in_=w_b_v[:, :, c0:c0 + 256]), writes=["wsb"], dma=True)
                s = i % 2
                cast2(wba[s], wsa, ("wba", s), "wsa", 8)
                cast2(wbb[s], wsb, ("wbb", s), "wsb", 8)
            if 0 <= i - 1 < NT:
                ii = i - 1
                s = ii % 2
                for ch in range(2):
                    cc = 2 * ii + ch
                    gs = cc % 2
                    S.op("sp", lambda e, cc=cc, gs=gs: e.dma_start(out=gta[gs], in_=gT[cc]), writes=[("gta", gs)], dma=True)
                    S.op("sp", lambda e, cc=cc, gs=gs: e.dma_start(out=gtb[gs], in_=gT[16 + cc]), writes=[("gtb", gs)], dma=True)
                    for tg in range(4):
                        pa = 2 * (r["p"] % 2); pb = pa + 1; r["p"] += 1
                        for k in range(8):
                            S.op("pe", lambda e, pa=pa, s=s, k=k, ch=ch, tg=tg: e.matmul(
                                psf[pa], lhsT=wba[s][:, k, ch * 128:(ch + 1) * 128], rhs=oaT[:, k, tg * 512:(tg + 1) * 512],
                                start=(k == 0), stop=(k == 7)), reads=[("wba", s, 0), ("wba", s, 1)], writes=[("ps", pa)])
                        for k in range(8):
                            S.op("pe", lambda e, pb=pb, s=s, k=k, ch=ch, tg=tg: e.matmul(
                                psf[pb], lhsT=wbb[s][:, k, ch * 128:(ch + 1) * 128], rhs=obT[:, k, tg * 512:(tg + 1) * 512],
                                start=(k == 0), stop=(k == 7)), reads=[("wbb", s, 0), ("wbb", s, 1)], writes=[("ps", pb)])
                        t = r["t"] % 2; r["t"] += 1
                        S.op("dve", lambda e, pa=pa, t=t, gs=gs, tg=tg: e.tensor_tensor(
                            out=tA[t], in0=psf[pa], in1=gta[gs][:, tg * 512:(tg + 1) * 512], op=ALU.mult),
                            reads=[("ps", pa), ("gta", gs)], writes=[("tA", t)])
                        S.op("dve", lambda e, pb=pb, t=t, gs=gs, tg=tg: e.tensor_tensor(
                            out=tB[t], in0=psf[pb], in1=gtb[gs][:, tg * 512:(tg + 1) * 512], op=ALU.mult),
                            reads=[("ps", pb), ("gtb", gs)], writes=[("tB", t)])
                        S.op("pool", lambda e, t=t, cc=cc, tg=tg: e.tensor_tensor(
                            out=mT[:, cc, tg * 512:(tg + 1) * 512], in0=tA[t], in1=tB[t], op=ALU.add),
                            reads=[("tA", t), ("tB", t)], writes=[("mT", cc, tg)])
        S.barrier()

        Dd = Arena(arena, smallbase, attbase)
        wso = Dd.f32(16 * 256).rearrange("p (k c) -> p k c", k=16)
        wbo = [Dd.bf16(16 * 256).rearrange("p (k c) -> p k c", k=16) for _ in range(2)]
        xres = [Dd.f32(2 * NTOK).rearrange("p (c t) -> p c t", c=2) for _ in range(2)]
        Dd2 = Arena(arena, cbase, LIM)
        htile = [Dd2.f32(512) for _ in range(4)]
        sqd = [Dd2.f32(512) for _ in range(2)]
        dacc = [Dd2.f32(512) for _ in range(4)]
        w_o_v = w_o.rearrange("(k p) c -> p k c", p=128)
        r = {"p": 0, "h": 0, "q": 0}
        NT = 8
        for i in range(NT + 2):
            if i < NT:
                c0 = i * 256
                s = i % 2
                S.op("sp", lambda e, c0=c0: e.dma_start(out=wso, in_=w_o_v[:, :, c0:c0 + 256]), writes=["wso"], dma=True)
                S.op("sp", lambda e, i=i, s=s: e.dma_start(out=xres[s], in_=xoT_v[:, 2 * i:2 * i + 2, :]),
                     writes=[("xres", s)], dma=True)
                cast2(wbo[s], wso, ("wbo", s), "wso", 16)
            if 0 <= i - 1 < NT:
                ii = i - 1
                s = ii % 2
                for ch in range(2):
                    cc = 2 * ii + ch
                    gs = cc % 2
                    S.op("sp", lambda e, cc=cc, gs=gs: e.dma_start(out=gta[gs], in_=gT[cc]), writes=[("gta", gs)], dma=True)
                    S.op("sp", lambda e, cc=cc, gs=gs: e.dma_start(out=gtb[gs], in_=gT[16 + cc]), writes=[("gtb", gs)], dma=True)
                    for tg in range(4):
                        pa = r["p"] % 4; r["p"] += 1
                        for k in range(16):
                            S.op("pe", lambda e, pa=pa, s=s, k=k, ch=ch, tg=tg: e.matmul(
                                psf[pa], lhsT=wbo[s][:, k, ch * 128:(ch + 1) * 128], rhs=mT[:, k, tg * 512:(tg + 1) * 512],
                                start=(k == 0), stop=(k == 15)), reads=[("wbo", s, 0), ("wbo", s, 1)], writes=[("ps", pa)])
                        hh = r["h"] % 4; r["h"] += 1
                        S.op("dve", lambda e, pa=pa, hh=hh, s=s, ch=ch, tg=tg: e.tensor_tensor(
                            out=htile[hh], in0=psf[pa], in1=xres[s][:, ch, tg * 512:(tg + 1) * 512], op=ALU.add),
                            reads=[("ps", pa), ("xres", s)], writes=[("ht", hh)])
                        S.op("sp", lambda e, hh=hh, cc=cc, tg=tg: e.dma_start(
                            out=hT[cc][:, tg * 512:(tg + 1) * 512], in_=htile[hh]), reads=[("ht", hh)], dma=True)
                        if cc == 0:
                            S.op("act", lambda e, hh=hh, tg=tg: e.activation(out=dacc[tg], in_=htile[hh], func=AF.Square),
                                 reads=[("ht", hh)], writes=[("dacc", tg)])
                        else:
                            q = r["q"] % 2; r["q"] += 1
                            S.op("act", lambda e, hh=hh, q=q: e.activation(out=sqd[q], in_=htile[hh], func=AF.Square),
                                 reads=[("ht", hh)], writes=[("sqd", q)])
                            S.op("pool", lambda e, q=q, tg=tg: e.tensor_tensor(out=dacc[tg], in0=dacc[tg], in1=sqd[q], op=ALU.add),
                                 reads=[("sqd", q), ("dacc", tg)], writes=[("dacc", tg)])
        for tg in range(4):
            S.op("pe", lambda e, tg=tg: e.matmul(psf[4 + tg], lhsT=ones[:, 0:128], rhs=dacc[tg], start=True, stop=True),
                 reads=[("dacc", tg)], writes=[("ps", 4 + tg)])
        for tg in range(4):
            S.op("act", lambda e, tg=tg: e.activation(out=rstd2[:, tg * 512:(tg + 1) * 512], in_=psf[4 + tg],
                                                      func=AF.Ln, scale=1.0 / D, bias=EPS),
                 reads=[("ps", 4 + tg)], writes=[("rstd2", tg)])
            S.op("act", lambda e, tg=tg: e.activation(out=rstd2[:, tg * 512:(tg + 1) * 512],
                                                      in_=rstd2[:, tg * 512:(tg + 1) * 512], func=AF.Exp, scale=-0.5),
                 reads=[("rstd2", tg)], writes=[("rstd2", tg)])
        S.barrier()

        hT_v = hT.rearrange("c p t -> p c t")
        w_g_v = w_g.rearrange("(k p) c -> p k c", p=128)
        w_u_v = w_u.rearrange("(k p) c -> p k c", p=128)
        w_d_v = w_d.rearrange("(k p) c -> p k c", p=128)
        out_ops = []
        for th in range(2):
            t0 = th * 1024
            E = Arena(arena, smallbase, LIM)
            hn = E.bf16(16 * 1024).rearrange("p (k t) -> p k t", k=16)
            hid = E.bf16(NFC * 1024).rearrange("p (f t) -> p f t", f=NFC)
            ebase = E.off
            def norm2(tt, hlb, hn=hn):
                nh = len(hlb)
                for k in range(16):
                    s = k % nh
                    S.op("sp", lambda e, k=k, s=s: e.dma_start(out=hlb[s], in_=hT[k][:, tt:tt + 1024]),
                         writes=[("hl", s)], dma=True)
                    S.op("dve", lambda e, k=k, s=s: e.scalar_tensor_tensor(
                        out=hn[:, k, :], in0=hlb[s], scalar=g2[:, k:k + 1], in1=rstd2[:, tt:tt + 1024],
                        op0=ALU.mult, op1=ALU.mult), reads=[("hl", s)], writes=[("hn", k)])
            if th == 0:
                norm2(0, [E.f32(1024) for _ in range(4)])
                S.barrier()
            E = Arena(arena, ebase, LIM)
            wsg = E.f32(16 * 256).rearrange("p (k c) -> p k c", k=16)
            wsu = E.f32(16 * 256).rearrange("p (k c) -> p k c", k=16)
            wbg = [E.bf16(16 * 256).rearrange("p (k c) -> p k c", k=16) for _ in range(2)]
            wbu = [E.bf16(16 * 256).rearrange("p (k c) -> p k c", k=16) for _ in range(2)]
            sg = [E.f32(512) for _ in range(2)]
            r = {"p": 0, "s": 0}
            NT = NFC // 2
            for i in range(NT + 1):
                if i < NT:
                    c0 = i * 256
                    S.op("sp", lambda e, c0=c0: e.dma_start(out=wsg, in_=w_g_v[:, :, c0:c0 + 256]), writes=["wsg"], dma=True)
                    S.op("sp", lambda e, c0=c0: e.dma_start(out=wsu, in_=w_u_v[:, :, c0:c0 + 256]), writes=["wsu"], dma=True)
                    s = i % 2
                    cast2(wbg[s], wsg, ("wbg", s), "wsg", 16)
                    cast2(wbu[s], wsu, ("wbu", s), "wsu", 16)
                if 0 <= i - 1 < NT:
                    ii = i - 1
                    s = ii % 2
                    for ch in range(2):
                        fc = 2 * ii + ch
                        for tg in range(2):
                            pg = 2 * (r["p"] % 4); pu = pg + 1; r["p"] += 1
                            for k in range(16):
                                S.op("pe", lambda e, pg=pg, s=s, k=k, ch=ch, tg=tg: e.matmul(
                                    psf[pg], lhsT=wbg[s][:, k, ch * 128:(ch + 1) * 128], rhs=hn[:, k, tg * 512:(tg + 1) * 512],
                                    start=(k == 0), stop=(k == 15)), reads=[("wbg", s, 0), ("wbg", s, 1)], writes=[("ps", pg)])
                            for k in range(16):
                                S.op("pe", lambda e, pu=pu, s=s, k=k, ch=ch, tg=tg: e.matmul(
                                    psf[pu], lhsT=wbu[s][:, k, ch * 128:(ch + 1) * 128], rhs=hn[:, k, tg * 512:(tg + 1) * 512],
                                    start=(k == 0), stop=(k == 15)), reads=[("wbu", s, 0), ("wbu", s, 1)], writes=[("ps", pu)])
                            q = r["s"] % 2; r["s"] += 1
                            S.op("act", lambda e, pg=pg, q=q: e.activation(out=sg[q], in_=psf[pg], func=AF.Silu),
                                 reads=[("ps", pg)], writes=[("sg", q)])
                            S.op("dve", lambda e, pu=pu, q=q, fc=fc, tg=tg: e.tensor_tensor(
                                out=hid[:, fc, tg * 512:(tg + 1) * 512], in0=psf[pu], in1=sg[q], op=ALU.mult),
                                reads=[("ps", pu), ("sg", q)], writes=[("hid", fc, tg)])
            S.barrier()
            E = Arena(arena, ebase, LIM)
            wsd = [E.f32(22 * 128).rearrange("p (k c) -> p k c", k=22) for _ in range(2)]
            wbd = [E.bf16(NFC * 128).rearrange("p (k c) -> p k c", k=NFC) for _ in range(2)]
            hres = [E.f32(1024) for _ in range(2)]
            ot = [E.f32(512) for _ in range(4)]
            hl2 = [E.f32(1024) for _ in range(2)]
            r = {"p": 0, "o": 0}
            NT = 16
            for i in range(NT + 2):
                if i == 3 and th == 0:
                    norm2(1024, hl2)
                if i < NT:
                    for hf in range(2):
                        S.op("sp", lambda e, i=i, hf=hf: e.dma_start(
                            out=wsd[hf], in_=w_d_v[:, hf * 22:(hf + 1) * 22, i * 128:(i + 1) * 128]),
                            writes=[("wsd", hf)], dma=True)
                    S.op("sp", lambda e, i=i, t0=t0: e.dma_start(out=hres[i % 2], in_=hT[i][:, t0:t0 + 1024]),
                         writes=[("hres", i % 2)], dma=True)
                    s = i % 2
                    S.op("dve", lambda e, s=s: e.tensor_copy(out=wbd[s][:, 0:22, :], in_=wsd[0]),
                         reads=[("wsd", 0)], writes=[("wbd", s, 0)])
                    S.op("act", lambda e, s=s: e.copy(out=wbd[s][:, 22:44, :], in_=wsd[1]),
                         reads=[("wsd", 1)], writes=[("wbd", s, 1)])
                if 0 <= i - 1 < NT:
                    cc = i - 1
                    s = cc % 2
                    for tg in range(2):
                        pa = r["p"] % 4; r["p"] += 1
                        for f in range(NFC):
                            S.op("pe", lambda e, pa=pa, s=s, f=f, tg=tg: e.matmul(
                                psf[pa], lhsT=wbd[s][:, f, :], rhs=hid[:, f, tg * 512:(tg + 1) * 512],
                                start=(f == 0), stop=(f == NFC - 1)),
                                reads=[("wbd", s, 0), ("wbd", s, 1)], writes=[("ps", pa)])
                        o = r["o"] % 4; r["o"] += 1
                        S.op("dve", lambda e, pa=pa, o=o, s=s, tg=tg: e.tensor_tensor(
                            out=ot[o], in0=psf[pa], in1=hres[s][:, tg * 512:(tg + 1) * 512], op=ALU.add),
                            reads=[("ps", pa), ("hres", s)], writes=[("ot", o)])
                        out_ops.append(S.op("sp", lambda e, o=o, cc=cc, tg=tg, t0=t0: e.dma_start(
                            out=outT[cc * 128:(cc + 1) * 128, t0 + tg * 512:t0 + (tg + 1) * 512], in_=ot[o]),
                            reads=[("ot", o)], dma=True))
            S.barrier()
        S.emit(block)
    return nc


_NC_CACHE = {}


def _host_consts():
    consts = {}
    slopes = np.exp2(-8.0 * (np.arange(4, dtype=np.float32) + 1.0) / 4).astype(np.float32)
    p = np.arange(128)[:, None, None]
    m = np.arange(32)[None, :, None]
    ti = np.arange(128)[None, None, :]
    for c in range(2):
        dist = np.abs(128 * (m + c - 2) + ti + p + 1).astype(np.float32)
        allowed = (2 * (1 - m) + (127 - p) // 64) <= (2 * c + ti // 64)
        eb = np.empty((4, 128, 32, 128), np.float32)
        for h in range(4):
            eb[h] = np.where(allowed, -slopes[h] * dist, np.float32(NEG))
        consts[("ebias", c)] = np.ascontiguousarray(eb.reshape(4, 128, 4096))
        tq = np.arange(128)[:, None]
        u = np.arange(256)[None, :]
        consts[("sbmask", c)] = np.ascontiguousarray((u + tq <= 255 - 128 * c).astype(np.float32))
    return consts


def _prep_inputs(x, norm1_g, w_in, q_norm_g, k_norm_g, lambda_q1, lambda_k1, lambda_q2, lambda_k2,
                 subln_g, w_branch_a, w_branch_b, w_out, norm2_g, w_ffn_gate, w_ffn_up, w_ffn_down):
    f = lambda a: np.ascontiguousarray(np.asarray(a, dtype=np.float32))
    consts = _host_consts()
    shared = {
        "w_in": f(w_in[0]), "w_a": f(w_branch_a[0]), "w_b": f(w_branch_b[0]), "w_o": f(w_out[0]),
        "w_g": f(w_ffn_gate[0]), "w_u": f(w_ffn_up[0]), "w_d": f(w_ffn_down[0]),
        "g1": f(np.asarray(norm1_g[0]).reshape(16, 128).T), "g2": f(np.asarray(norm2_g[0]).reshape(16, 128).T),
        "gq": f(np.asarray(q_norm_g[0]).reshape(128, 1)), "gk": f(np.asarray(k_norm_g[0]).reshape(128, 1)),
        "lamv": f(np.tile(np.concatenate([np.asarray(lambda_q1[0]), np.asarray(lambda_k1[0]),
                                          np.asarray(lambda_q2[0]), np.asarray(lambda_k2[0])])[None, :], (128, 1))),
        "gsub": f(np.tile(np.asarray(subln_g[0])[None, :], (128, 1))),
    }
    x = np.asarray(x, dtype=np.float32)
    in_maps = []
    for core in range(8):
        b, c = core // 2, core % 2
        xb = x[b]
        own = xb.reshape(32, 128, D)[c::2].reshape(NTOK, D)
        m = dict(shared)
        m["xoT"] = np.ascontiguousarray(own.T)
        m["xfT"] = np.ascontiguousarray(xb[::-1].T)
        m["ebias"] = consts[("ebias", c)]
        m["sbmask"] = consts[("sbmask", c)]
        in_maps.append(m)
    return in_maps


def kernel(**inputs):
    in_maps = _prep_inputs(**inputs)
    if "nc" not in _NC_CACHE:
        _NC_CACHE["nc"] = build_program()
    nc = _NC_CACHE["nc"]
    res = run_bass_kernel_spmd(nc, in_maps, core_ids=list(range(8)))
    out = np.empty((4, SEQ, D), np.float32)
    for core in range(8):
        b, c = core // 2, core % 2
        o = np.asarray(res.results[core]["outT"]).T.reshape(16, 128, D)
        out[b].reshape(32, 128, D)[c::2] = o
    return out
```
